# Optimizing a Trainium2 kernel written in Bass

```python
import jax, jax.numpy as jnp
from jax import lax
import numpy as np

D_MODEL = 1024
BATCH = 4
SEQ = 8192
DEPTH = 4

N_MIXERS = 2
N_LAYERS_A = (DEPTH + N_MIXERS - 1) // N_MIXERS
N_LAYERS_B = DEPTH // N_MIXERS
LRU_WIDTH = D_MODEL
LRU_HEADS = 4
LRU_HEAD_DIM = LRU_WIDTH // LRU_HEADS
CONV_WIDTH = 4
LRU_C = 8.0
MIN_RAD = 0.9
MAX_RAD = 0.999
POOL_WINDOWS = (2, 4, 8, 16)
POOL_GROUPS = len(POOL_WINDOWS)
POOL_GROUP_DIM = D_MODEL // POOL_GROUPS
D_FF = 4 * D_MODEL
N_MOD = 6
EPS = 1e-6

kernel_name = "hybrid_rglru_multiscale_pool_adaln"


def rms_norm(x, g):
    xf = x.astype(jnp.float32)
    y = xf * lax.rsqrt(jnp.mean(xf * xf, axis=-1, keepdims=True) + EPS)
    return (y * g.astype(jnp.float32)).astype(x.dtype)


def modulate(h, shift, scale):
    return h * (1.0 + scale[:, None, :]) + shift[:, None, :]


def causal_depthwise_conv(x, w, b):
    s = x.shape[1]
    xp = jnp.pad(x, ((0, 0), (CONV_WIDTH - 1, 0), (0, 0)))
    y = xp[:, 0:s] * w[0]
    for k in range(1, CONV_WIDTH):
        y = y + xp[:, k:k + s] * w[k]
    return y + b


def _lru_combine(left, right):
    a1, b1 = left
    a2, b2 = right
    return a1 * a2, a2 * b1 + b2


def block_diag_linear(x, w, b):
    bsz, s, _ = x.shape
    xh = x.reshape(bsz, s, LRU_HEADS, LRU_HEAD_DIM)
    y = jnp.einsum("bshi,hij->bshj", xh, w) + b
    return y.reshape(bsz, s, LRU_WIDTH)


def rg_lru(x, w_a, b_a, w_x, b_x, lam):
    gate_r = jax.nn.sigmoid(block_diag_linear(x, w_a, b_a)).astype(jnp.float32)
    gate_i = jax.nn.sigmoid(block_diag_linear(x, w_x, b_x)).astype(jnp.float32)
    log_a = LRU_C * gate_r * jax.nn.log_sigmoid(lam.astype(jnp.float32))
    a = jnp.exp(log_a)
    mult = jnp.sqrt(-jnp.expm1(2.0 * log_a))
    u = mult * (gate_i * x.astype(jnp.float32))
    _, h = lax.associative_scan(_lru_combine, (a, u), axis=1)
    return h.astype(x.dtype)


def recurrent_mixer(h, w_y, b_y, w_in, b_in, conv_w, conv_b, w_a, b_a, w_x, b_x, lam, w_out, b_out):
    gate_branch = jax.nn.gelu(jnp.einsum("bsd,dw->bsw", h, w_y) + b_y)
    xr = jnp.einsum("bsd,dw->bsw", h, w_in) + b_in
    xr = causal_depthwise_conv(xr, conv_w, conv_b)
    xr = rg_lru(xr, w_a, b_a, w_x, b_x, lam)
    return jnp.einsum("bsw,wd->bsd", xr * gate_branch, w_out) + b_out


def pool_mixer(h, w_pool, pool_scale):
    bsz, s, _ = h.shape
    hf = h.astype(jnp.float32)
    counts = jnp.arange(1, s + 1, dtype=jnp.float32)
    outs = []
    for g, win in enumerate(POOL_WINDOWS):
        xg = hf[..., g * POOL_GROUP_DIM:(g + 1) * POOL_GROUP_DIM]
        cs = jnp.cumsum(xg, axis=1)
        cs_lag = jnp.pad(cs, ((0, 0), (win, 0), (0, 0)))[:, :s]
        mean = (cs - cs_lag) / jnp.minimum(counts, float(win))[None, :, None]
        outs.append(mean - xg)
    pooled = jnp.stack(outs, axis=2).astype(h.dtype)
    mixed = jnp.einsum("bsgi,gij->bsgj", pooled, w_pool).reshape(bsz, s, D_MODEL)
    return mixed * pool_scale


def sq_relu_mlp(h, w1, w2):
    u = jax.nn.relu(jnp.einsum("bsd,df->bsf", h, w1))
    return jnp.einsum("bsf,fd->bsd", u * u, w2)


def setup_inputs(seed: int = 0) -> dict:
    key = jax.random.key(seed)
    ks = jax.random.split(key, 26)
    f32 = jnp.float32
    nrm = lambda k, shape, s: (jax.random.normal(k, shape, f32) * s)
    d, w, hd, na, nb = D_MODEL, LRU_WIDTH, LRU_HEAD_DIM, N_LAYERS_A, N_LAYERS_B
    rad = jnp.sqrt(jax.random.uniform(ks[15], (na, w), f32, MIN_RAD ** 2, MAX_RAD ** 2))
    return {
        "x": nrm(ks[0], (BATCH, SEQ, d), 1.0),
        "c": nrm(ks[1], (BATCH, d), 1.0),
        "w_mod": nrm(ks[2], (DEPTH, d, N_MOD * d), 0.5 * d ** -0.5),
        "b_mod": nrm(ks[3], (DEPTH, N_MOD * d), 0.02),
        "norm_mix_g": 1.0 + nrm(ks[4], (DEPTH, d), 0.05),
        "norm_ffn_g": 1.0 + nrm(ks[5], (DEPTH, d), 0.05),
        "lru_w_y": nrm(ks[6], (na, d, w), d ** -0.5),
        "lru_b_y": nrm(ks[7], (na, w), 0.02),
        "lru_w_in": nrm(ks[8], (na, d, w), d ** -0.5),
        "lru_b_in": nrm(ks[9], (na, w), 0.02),
        "lru_conv_w": nrm(ks[10], (na, CONV_WIDTH, w), CONV_WIDTH ** -0.5),
        "lru_conv_b": nrm(ks[11], (na, w), 0.02),
        "lru_w_a": nrm(ks[12], (na, LRU_HEADS, hd, hd), hd ** -0.5),
        "lru_b_a": nrm(ks[13], (na, LRU_HEADS, hd), 0.02),
        "lru_w_x": nrm(ks[14], (na, LRU_HEADS, hd, hd), hd ** -0.5),
        "lru_b_x": nrm(ks[16], (na, LRU_HEADS, hd), 0.02),
        "lru_lambda": jnp.log(rad) - jnp.log1p(-rad),
        "lru_w_out": nrm(ks[17], (na, w, d), w ** -0.5),
        "lru_b_out": nrm(ks[18], (na, d), 0.02),
        "pool_w": nrm(ks[19], (nb, POOL_GROUPS, POOL_GROUP_DIM, POOL_GROUP_DIM), POOL_GROUP_DIM ** -0.5),
        "pool_scale": 1.0 + nrm(ks[20], (nb, d), 0.1),
        "ffn_w1": nrm(ks[21], (DEPTH, d, D_FF), d ** -0.5),
        "ffn_w2": nrm(ks[22], (DEPTH, D_FF, d), D_FF ** -0.5),
        "final_norm_g": 1.0 + nrm(ks[23], (d,), 0.05),
    }


def reference(x, c, w_mod, b_mod, norm_mix_g, norm_ffn_g, lru_w_y, lru_b_y, lru_w_in, lru_b_in,
              lru_conv_w, lru_conv_b, lru_w_a, lru_b_a, lru_w_x, lru_b_x, lru_lambda, lru_w_out,
              lru_b_out, pool_w, pool_scale, ffn_w1, ffn_w2, final_norm_g):
    cond = jax.nn.silu(c)
    for i in range(DEPTH):
        mod = jnp.einsum("bd,de->be", cond, w_mod[i]) + b_mod[i]
        sh_m, sc_m, gt_m, sh_f, sc_f, gt_f = jnp.split(mod, N_MOD, axis=-1)
        h = modulate(rms_norm(x, norm_mix_g[i]), sh_m, sc_m)
        j = i // N_MIXERS
        if i % N_MIXERS == 0:
            y = recurrent_mixer(h, lru_w_y[j], lru_b_y[j], lru_w_in[j], lru_b_in[j],
                                lru_conv_w[j], lru_conv_b[j], lru_w_a[j], lru_b_a[j],
                                lru_w_x[j], lru_b_x[j], lru_lambda[j], lru_w_out[j], lru_b_out[j])
        else:
            y = pool_mixer(h, pool_w[j], pool_scale[j])
        x = x + gt_m[:, None, :] * y
        h = modulate(rms_norm(x, norm_ffn_g[i]), sh_f, sc_f)
        x = x + gt_f[:, None, :] * sq_relu_mlp(h, ffn_w1[i], ffn_w2[i])
    return rms_norm(x, final_norm_g)
```

```python
import numpy as np
from contextlib import ExitStack
import concourse.bass as bass
import concourse.mybir as mybir
from concourse.bass_utils import run_bass_kernel_spmd

F32 = mybir.dt.float32
BF16 = mybir.dt.bfloat16
AF = mybir.ActivationFunctionType
ALU = mybir.AluOpType

D = 1024
NCH = 8
FF = 4096
HC = 32
T = 512
DEPTH = 4
SEQ = 8192
BATCH = 4
EPS = 1e-6
NSLOT = 3
SLOTW = 8192
HALO = 16
SEM_CHUNK = 6000
SAME_ENGINE_FULL = True


def vec_index():
    names = []
    for i in range(DEPTH):
        names += [("gmix", i), ("gffn", i)] + [("bmod", i, k) for k in range(6)]
    for j in range(2):
        names += [(n, j) for n in ("b_y", "b_in", "cw0", "cw1", "cw2", "cw3", "conv_b",
                                   "b_a", "b_x", "lam", "b_out")]
    for j in range(2):
        names.append(("pscale", j))
    names.append(("fg",))
    return {n: k for k, n in enumerate(names)}


VIDX = vec_index()
NV = len(VIDX)


class Op:
    __slots__ = ("eng", "emit", "deps", "dma", "wkeys", "signal", "sem", "ticket", "inc")


class Sched:
    ENG = ("pe", "act", "dve", "pool", "sp")

    def __init__(self):
        self.q = {e: [] for e in self.ENG}
        self.lastw = {}
        self.readers = {}
        self.extra_reads = ()

    def add(self, eng, emit, reads=(), writes=(), dma=None, inc=16):
        op = Op()
        op.eng, op.emit, op.dma = eng, emit, dma
        op.inc = inc
        reads = tuple(reads) + tuple(self.extra_reads)
        op.wkeys = set(writes)
        op.signal = dma is not None
        op.sem = None
        op.ticket = None
        rset = set(reads)
        deps = set()
        for k in reads:
            w = self.lastw.get(k)
            if w is not None:
                deps.add(w)
        for k in writes:
            w = self.lastw.get(k)
            if w is not None:
                deps.add(w)
            for r in self.readers.get(k, ()):
                deps.add(r)
        fdeps = []
        for d in deps:
            if d.eng == eng and d.dma is None and dma is None:
                if eng == "pe":
                    continue
                if not SAME_ENGINE_FULL and not (d.wkeys & rset):
                    continue
            d.signal = True
            fdeps.append(d)
        op.deps = fdeps
        for k in reads:
            self.readers.setdefault(k, []).append(op)
        for k in writes:
            self.lastw[k] = op
            self.readers[k] = []
        self.q[eng].append(op)
        return op

    def assign(self, nc, stack):
        self.finals = []
        dma_sems = {}
        dma_cnt = {}
        for e in self.ENG:
            cnt = 0
            sem = None
            for op in self.q[e]:
                if op.dma is not None:
                    if op.dma not in dma_sems:
                        dma_sems[op.dma] = stack.enter_context(nc.semaphore("d_%s" % str(op.dma)))
                        dma_cnt[op.dma] = 0
                    dma_cnt[op.dma] += op.inc
                    op.sem, op.ticket = dma_sems[op.dma], dma_cnt[op.dma]
                elif op.signal:
                    if sem is None or cnt >= SEM_CHUNK:
                        sem = stack.enter_context(nc.semaphore("e_%s_%d" % (e, len(self.finals))))
                        self.finals.append(sem)
                        cnt = 0
                    cnt += 1
                    op.sem, op.ticket, op.inc = sem, cnt, 1
        self.dma_final = [(dma_sems[k], dma_cnt[k]) for k in dma_sems]

    def emit_engine(self, e, engobj, tail_waits=()):
        waited = {}
        for op in self.q[e]:
            need = {}
            for d in op.deps:
                if need.get(d.sem, (None, 0))[1] < d.ticket:
                    need[d.sem] = (d.sem, d.ticket)
            for sem, t in need.values():
                if waited.get(sem, 0) < t:
                    engobj.wait_ge(sem, t)
                    waited[sem] = t
            ins = op.emit(engobj)
            if op.sem is not None:
                if op.dma is not None and op.inc == 1:
                    ins.then_inc(op.sem)
                else:
                    ins.then_inc(op.sem, op.inc)
        for sem, t in tail_waits:
            if waited.get(sem, 0) < t:
                engobj.wait_ge(sem, t)


def build_program(NT, nlayers=2, debug=False, LAG=2):
    S = NT * T
    NL = nlayers
    NP = NT + LAG
    nc = bass.Bass("TRN2", target_bir_lowering=False)
    sch = Sched()
    stack = ExitStack()

    def din(name, shape, dt=F32):
        return nc.dram_tensor(name, list(shape), dt, kind="ExternalInput").ap()

    xT = din("xT", [D, S])
    cvec = din("cvec", [128, NCH])
    vecs_d = din("vecs", [128, NV * NCH])
    w_mod = din("w_mod", [NL, D, 6 * D])
    flags_d = din("flags", [128, 4])
    w_y = din("lru_w_y", [1, D, D])
    w_in = din("lru_w_in", [1, D, D])
    w_out = din("lru_w_out", [1, D, D])
    w_a = din("lru_w_a", [1, 4, 256, 256])
    w_x = din("lru_w_x", [1, 4, 256, 256])
    pool_w = din("pool_w", [1, 4, 256, 256])
    w1 = din("ffn_w1", [NL, D, FF])
    w2 = din("ffn_w2", [NL, FF, D])
    outT = nc.dram_tensor("outT", [D, S], F32, kind="ExternalOutput").ap()
    NCC = NP - LAG
    cc_in = [nc.dram_tensor("cc_in%d" % k, [256, NCH * T], F32, kind="Internal").ap() for k in range(NCC)]
    cc_out = [nc.dram_tensor("cc_out%d" % k, [128, NCH * T], F32, kind="Internal").ap() for k in range(NCC)]
    GROUPS = [[0, 1], [2, 3], [4, 5], [6, 7]]

    def sb(name, shape, dt):
        return stack.enter_context(nc.sbuf_tensor(name, list(shape), dt))

    wsl = sb("wsl", [128, NSLOT, SLOTW], BF16)
    xbuf = [sb("xbuf%d" % i, [128, NCH, T], F32) for i in range(2)]
    xn = sb("xn", [128, NCH, T], BF16)
    cvb = sb("cvb", [128, NCH, T], BF16)
    gate = sb("gate", [128, NCH, T], BF16)
    big = sb("big", [128, NCH, HALO + T], F32)
    r2 = sb("r2", [128, HC, T], BF16)
    NSQ = 10
    sq = sb("sq", [128, NSQ, T], BF16)
    stdt = sb("stdt", [128, 2, T], F32)
    rstd = sb("rstd", [128, 2, T], F32)

    tn = sb("tn", [128, 2, T], F32)
    xr = sb("xr", [128, 4, 4 + T], F32)
    g_tha = sb("g_tha", [128, 2, HALO + T], F32)
    g_thx = sb("g_thx", [128, 2, HALO + T], F32)
    g_a = sb("g_a", [128, 2, T], F32)
    g_m = sb("g_m", [128, 2, T], F32)
    pl = g_tha
    plb = g_thx
    vecs = sb("vecs_s", [128, NV, NCH], F32)
    ND = DEPTH * 8 + 2 * 5 + 2
    der = sb("der", [128, ND, NCH], F32)
    cvt = sb("cvt", [128, NCH], F32)
    cond = sb("cond", [128, NCH], F32)
    ctmp = sb("ctmp", [128, NCH], F32)
    kcol = sb("kcol", [128, 4], F32)
    ones = sb("ones", [128, 128], BF16)
    hst = sb("hst", [128, 2, NCH], F32)
    chalo = sb("chalo", [128, 2, NCH, 4], F32)
    vhalo = sb("vhalo", [128, 2, NCH, HALO], F32)
    invc = sb("invc", [128, 4, HALO], F32)
    tabs = sb("tabs", [128, 2, 4, HALO], F32)
    flg = sb("flg", [128, 4], F32)
    ps = stack.enter_context(nc.psum_tensor("ps", [128, 8, T], F32))

    def dmod(i, part):
        return i * 8 + part
    def dlru(j, k):
        return DEPTH * 8 + j * 5 + k
    def dpool(j):
        return DEPTH * 8 + 10 + j

    def V(name, c):
        return vecs[:, VIDX[name], c:c + 1]
    def Dc(idx, c):
        return der[:, idx, c:c + 1]

    bank_ctr = [0]
    def next_bank():
        b = bank_ctr[0] % 6
        bank_ctr[0] += 1
        return b

    scr = {}
    def scr_piece(key):
        if key not in scr:
            scr[key] = nc.dram_tensor("scr_%d" % len(scr), [128, SLOTW], BF16, kind="Internal").ap()
        return scr[key]

    A = sch.add
    A("sp", lambda e: e.dma_start(out=vecs[:].rearrange("p a b -> p (a b)"), in_=vecs_d), writes=[("vecs",)], dma="vecs")
    A("sp", lambda e: e.dma_start(out=cvt[:], in_=cvec), writes=[("cvt",)], dma="cvt")
    A("pool", lambda e: e.memset(kcol[:, 0:1], EPS), writes=[("kcol",)])
    A("pool", lambda e: e.memset(kcol[:, 1:2], 0.25 + 5e-7), writes=[("kcol",)])
    A("pool", lambda e: e.memset(kcol[:, 2:3], 1.0), writes=[("kcol",)])
    A("pool", lambda e: e.memset(kcol[:, 3:4], 0.0), writes=[("kcol",)])
    A("pool", lambda e: e.memset(ones[:], 1.0 / D), writes=[("ones",)])
    A("pool", lambda e: e.memset(hst[:], 0.0), writes=[("hst",)])
    A("pool", lambda e: e.memset(chalo[:], 0.0), writes=[("chalo",)])
    A("pool", lambda e: e.memset(vhalo[:], 0.0), writes=[("vhalo",)])
    for g in range(4):
        win = 2 << g
        for t in range(HALO):
            val = 1.0 / min(t + 1, win)
            A("pool", (lambda e, g=g, t=t, val=val: e.memset(invc[:, g, t:t + 1], val)), writes=[("invc",)])
    A("sp", lambda e: e.dma_start(out=flg[:], in_=flags_d), writes=[("flg",)], dma="flg")
    for w_ in range(2):
        for g in range(4):
            win = 2 << g
            A("dve", (lambda e, w_=w_, g=g, win=win: e.tensor_scalar(out=tabs[:, w_, g, :], in0=invc[:, g, :], scalar1=-1.0 / win, scalar2=None, op0=ALU.add)),
              reads=[("invc",)], writes=[("tabs",)])
            A("dve", (lambda e, w_=w_, g=g: e.tensor_scalar(out=tabs[:, w_, g, :], in0=tabs[:, w_, g, :], scalar1=flg[:, w_:w_ + 1], scalar2=None, op0=ALU.mult)),
              reads=[("tabs",), ("flg",)], writes=[("tabs",)])
            A("dve", (lambda e, w_=w_, g=g, win=win: e.tensor_scalar(out=tabs[:, w_, g, :], in0=tabs[:, w_, g, :], scalar1=1.0 / win, scalar2=None, op0=ALU.add)),
              reads=[("tabs",)], writes=[("tabs",)])
    xT3 = xT.rearrange("(k p) t -> p k t", p=128)
    oT3 = outT.rearrange("(k p) t -> p k t", p=128)
    for k in range(NCC):
        tsrc = min(k + LAG, NT - 1)
        A("sp", (lambda e, k=k, tsrc=tsrc: e.dma_start(out=cc_in[k][0:128, :].rearrange("p (c t) -> p c t", c=NCH),
                                                        in_=xT3[:, :, tsrc * T:(tsrc + 1) * T])),
          writes=[("ccin0",)], dma="ccin0")
    A("act", lambda e: e.activation(out=ctmp[:], in_=cvt[:], func=AF.Tanh, scale=0.5), reads=[("cvt",)], writes=[("ctmp",)])
    A("dve", lambda e: e.scalar_tensor_tensor(out=cond[:], in0=ctmp[:], scalar=1.0, in1=cvt[:], op0=ALU.add, op1=ALU.mult),
      reads=[("ctmp",), ("cvt",)], writes=[("cond",)])
    A("dve", lambda e: e.tensor_scalar(out=cond[:], in0=cond[:], scalar1=0.5, scalar2=None, op0=ALU.mult),
      reads=[("cond",)], writes=[("cond",)])

    slot_ctr = [0]
    def next_slot():
        s = slot_ctr[0] % NSLOT
        slot_ctr[0] += 1
        return s

    MODW = 512
    mod_tasks = []
    def mod_layer(i):
        def piece(pc, i=i):
            s = next_slot()
            stg = wsl[:, s, :].bitcast(F32)
            src = w_mod[i].rearrange("(k p) e -> p k e", p=128)[:, :, pc * MODW:(pc + 1) * MODW]
            A("sp", (lambda e, stg=stg, src=src: e.dma_start(out=stg.rearrange("p (k e) -> p k e", k=NCH), in_=src)),
              writes=[("wslot", s)], dma=("slot", s))
            for jj in range(MODW // 128):
                jcol = pc * (MODW // 128) + jj
                for kc in range(NCH):
                    A("pe", (lambda e, stg=stg, kc=kc, jj=jj, jcol=jcol: e.matmul(
                        ps[:, 7, jcol:jcol + 1], lhsT=stg[:, kc * MODW + jj * 128: kc * MODW + (jj + 1) * 128],
                        rhs=cond[:, kc:kc + 1], start=(kc == 0), stop=(kc == NCH - 1))),
                      reads=[("wslot", s), ("cond",)], writes=[("psum", 7)])
        for pc in range(6 * D // MODW):
            mod_tasks.append(lambda pc=pc, piece=piece: piece(pc))
        def rest(i=i):
            i0 = VIDX[("bmod", i, 0)]
            A("dve", (lambda e, i=i, i0=i0: e.tensor_tensor(
                out=der[:, dmod(i, 0):dmod(i, 0) + 6, :].rearrange("p a b -> p (a b)"), in0=ps[:, 7, 0:48],
                in1=vecs[:, i0:i0 + 6, :].rearrange("p a b -> p (a b)"), op=ALU.add)),
              reads=[("psum", 7), ("vecs",)], writes=[("der",)])
            A("dve", (lambda e, i=i: e.scalar_tensor_tensor(out=der[:, dmod(i, 6), :], in0=der[:, dmod(i, 1), :], scalar=1.0,
                                                             in1=vecs[:, VIDX[("gmix", i)], :], op0=ALU.add, op1=ALU.mult)),
              reads=[("der",), ("vecs",)], writes=[("der",)])
            A("dve", (lambda e, i=i: e.scalar_tensor_tensor(out=der[:, dmod(i, 7), :], in0=der[:, dmod(i, 4), :], scalar=1.0,
                                                             in1=vecs[:, VIDX[("gffn", i)], :], op0=ALU.add, op1=ALU.mult)),
              reads=[("der",), ("vecs",)], writes=[("der",)])
            j = i // 2
            if i % 2 == 0:
                A("dve", (lambda e, j=j: e.tensor_scalar(out=der[:, dlru(j, 0), :], in0=vecs[:, VIDX[("b_a", j)], :], scalar1=0.5, scalar2=None, op0=ALU.mult)),
                  reads=[("vecs",)], writes=[("der",)])
                A("dve", (lambda e, j=j: e.tensor_scalar(out=der[:, dlru(j, 1), :], in0=vecs[:, VIDX[("b_x", j)], :], scalar1=0.5, scalar2=None, op0=ALU.mult)),
                  reads=[("vecs",)], writes=[("der",)])
                A("act", (lambda e, j=j: e.activation(out=ctmp[:], in_=vecs[:, VIDX[("lam", j)], :], func=AF.Exp, scale=-1.0)),
                  reads=[("vecs",), ("ctmp",)], writes=[("ctmp",)])
                A("act", (lambda e: e.activation(out=ctmp[:], in_=ctmp[:], func=AF.Ln, bias=kcol[:, 2:3], scale=1.0)),
                  reads=[("ctmp",), ("kcol",)], writes=[("ctmp",)])
                A("dve", (lambda e, j=j: e.tensor_scalar(out=der[:, dlru(j, 2), :], in0=ctmp[:], scalar1=-4.0, scalar2=None, op0=ALU.mult)),
                  reads=[("ctmp",)], writes=[("der",)])
                A("dve", (lambda e, j=j: e.tensor_scalar(out=der[:, dlru(j, 3), :], in0=ctmp[:], scalar1=-8.0, scalar2=None, op0=ALU.mult)),
                  reads=[("ctmp",)], writes=[("der",), ("ctmp",)])
                A("dve", (lambda e, i=i, j=j: e.tensor_tensor(out=der[:, dlru(j, 4), :], in0=der[:, dmod(i, 2), :],
                                                               in1=vecs[:, VIDX[("b_out", j)], :], op=ALU.mult)),
                  reads=[("der",), ("vecs",)], writes=[("der",)])
            else:
                A("dve", (lambda e, i=i, j=j: e.tensor_tensor(out=der[:, dpool(j), :], in0=der[:, dmod(i, 2), :],
                                                               in1=vecs[:, VIDX[("pscale", j)], :], op=ALU.mult)),
                  reads=[("der",), ("vecs",)], writes=[("der",)])

        mod_tasks.append(rest)
    for i in range(nlayers):
        mod_layer(i)

    cast_rr = [0]
    ostg_ctr = [0]
    piece_sub = {}
    conv_tasks = []

    def convert(src, dst, shape3, pkey):
        sub = (pkey, len(piece_sub.setdefault(pkey, [])))
        piece_sub[pkey].append(sub)
        conv_tasks.append(lambda: convert_now(src, dst, shape3, sub))

    def convert_now(src, dst, shape3, sub):
        a, b = shape3
        n = a * b
        s = next_slot()
        stg = wsl[:, s, :].bitcast(F32)[:, 0:n]
        o = ostg_ctr[0] % 2
        ostg_ctr[0] += 1
        ost = r2[:, o * 8:(o + 1) * 8, :].rearrange("p a b -> p (a b)")[:, 0:n]
        okeys = [("r2", o * 8 + q) for q in range(8)]
        A("sp", (lambda e: e.dma_start(out=stg.rearrange("p (a b) -> p a b", a=a), in_=src)),
          writes=[("wslot", s)], dma=("slot", s))
        ce = ("dve", "act", "pool")[cast_rr[0] % 3]
        cast_rr[0] += 1
        if ce == "act":
            A("act", (lambda e: e.activation(out=ost, in_=stg, func=AF.Copy)), reads=[("wslot", s)], writes=okeys)
        else:
            A(ce, (lambda e: e.tensor_copy(out=ost, in_=stg)), reads=[("wslot", s)], writes=okeys)
        A("sp", (lambda e: e.dma_start(out=dst, in_=ost)), reads=okeys, writes=[("scr", sub)], dma=("ostg", o))

    def conv_dense(wap, key_fn, ncols_total):
        for pj in range(ncols_total // 1024):
            dstp = scr_piece(key_fn(pj))
            for half in range(2):
                src = wap.rearrange("(k p) n -> p k n", p=128)[:, half * 4:(half + 1) * 4, pj * 1024:(pj + 1) * 1024]
                convert(src, dstp[:, half * 4096:(half + 1) * 4096], (4, 1024), key_fn(pj))

    def conv_w2(wap, key_fn):
        for pj in range(4):
            dstp = scr_piece(key_fn(pj))
            for half in range(2):
                src = wap.rearrange("(k p) n -> p k n", p=128)[:, half * 16:(half + 1) * 16, pj * 256:(pj + 1) * 256]
                convert(src, dstp[:, half * 4096:(half + 1) * 4096], (16, 256), key_fn(pj))

    def conv_bd(wap, dst, pkey):
        src = wap.rearrange("h (k p) n -> p (h k) n", p=128)
        convert(src, dst, (8, 256), pkey)

    for i in range(nlayers):
        j = i // 2
        if i % 2 == 0:
            conv_dense(w_y[j], lambda pj, i=i: (i, "wy"), 1024)
            conv_dense(w_in[j], lambda pj, i=i: (i, "win"), 1024)
            pbd = scr_piece((i, "wax"))
            conv_bd(w_a[j], pbd[:, 0:2048], (i, "wax"))
            conv_bd(w_x[j], pbd[:, 2048:4096], (i, "wax"))
            conv_dense(w_out[j], lambda pj, i=i: (i, "wout"), 1024)
        else:
            pbd = scr_piece((i, "pw"))
            conv_bd(pool_w[j], pbd[:, 0:2048], (i, "pw"))
        conv_dense(w1[i], lambda pj, i=i: (i, "w1", pj), FF)
        conv_w2(w2[i], lambda pj, i=i: (i, "w2", pj))

    while mod_tasks or conv_tasks:
        if mod_tasks:
            mod_tasks.pop(0)()
        for _ in range(2):
            if conv_tasks:
                conv_tasks.pop(0)()

    sch.extra_reads = (("der",), ("vecs",), ("kcol",), ("ones",), ("invc",))


    def use_piece(pkey, n=SLOTW):
        s = next_slot()
        dstp = scr_piece(pkey)
        A("sp", (lambda e: e.dma_start(out=wsl[:, s, 0:n], in_=dstp[:, 0:n])),
          reads=[("scr", sub) for sub in piece_sub[pkey]], writes=[("wslot", s)], dma=("slot", s))
        return s

    ctr = {"tn": 0, "sq": 0, "xr": 0}
    def rot(name):
        v = ctr[name] % (4 if name == "xr" else 2)
        ctr[name] += 1
        return v

    pe_pend = []
    pend_bufs = []
    PEND_DELAY = 2

    def drain_pend(keep):
        while len(pe_pend) > keep:
            pe_pend.pop(0)()
            pend_bufs.pop(0)

    def mm_group(b, lhs_fn, rhs_fn, nk, rkeys):
        for kc in range(nk):
            A("pe", (lambda e, kc=kc: e.matmul(ps[:, b, :], lhsT=lhs_fn(kc), rhs=rhs_fn(kc),
                                               start=(kc == 0), stop=(kc == nk - 1))),
              reads=rkeys(kc), writes=[("psum", b)])
        if len(pe_pend) > PEND_DELAY:
            pe_pend.pop(0)()
            pend_bufs.pop(0)

    stat_ctr = [0]

    def stat_begin():
        k = stat_ctr[0] % 2
        stat_ctr[0] += 1
        return {"k": k, "bank": 6 + k, "n": 0}

    def stat_chunk(ctx, bi, c):
        xb = xbuf[bi]
        si = ctr["sq"] % NSQ
        ctr["sq"] += 1
        assert si not in pend_bufs, "sq buffer still pending"
        pend_bufs.append(si)
        first = ctx["n"] == 0
        last = ctx["n"] == NCH - 1
        ctx["n"] += 1
        bank = ctx["bank"]
        A("act", (lambda e: e.activation(out=sq[:, si, :], in_=xb[:, c, :], func=AF.Square)),
          reads=[("x", bi, c)], writes=[("sq", si)])
        pe_pend.append(lambda: A("pe", (lambda e: e.matmul(ps[:, bank, :], lhsT=ones[:], rhs=sq[:, si, :], start=first, stop=last)),
                                 reads=[("sq", si)], writes=[("psum", bank)]))

    def stat_finish(ctx):
        k, bank = ctx["k"], ctx["bank"]
        assert ctx["n"] == NCH
        drain_pend(0)
        A("act", (lambda e: e.activation(out=stdt[:, k, :], in_=ps[:, bank, :], func=AF.Sqrt, bias=kcol[:, 0:1], scale=1.0)),
          reads=[("psum", bank)], writes=[("stdt", k)])
        A("dve", (lambda e: e.reciprocal(out=rstd[:, k, :], in_=stdt[:, k, :])),
          reads=[("stdt", k)], writes=[("rstd", k)])

    def norm_apply(ctx, bi, scfn, bifn, dstfn, keyfn):
        xb = xbuf[bi]
        k = ctx["k"]
        for c in range(NCH):
            ti = rot("tn")
            A("dve", (lambda e, c=c, ti=ti: e.tensor_tensor(out=tn[:, ti, :], in0=xb[:, c, :], in1=rstd[:, k, :], op=ALU.mult)),
              reads=[("x", bi, c), ("rstd", k)], writes=[("tn", ti)])
            A("act", (lambda e, c=c, ti=ti: e.activation(out=dstfn(c), in_=tn[:, ti, :], func=AF.Identity,
                                                          scale=scfn(c), bias=bifn(c))),
              reads=[("tn", ti)], writes=([keyfn(c), ("r2", keyfn(c)[1] + 1)] if keyfn(c)[0] == "r2" else [keyfn(c)]))

    zero_b = lambda c: kcol[:, 3:4]

    def lru_norm(i, bi, ctx):
        norm_apply(ctx, bi, lambda c: Dc(dmod(i, 6), c), lambda c: Dc(dmod(i, 0), c),
                   lambda c: xn[:, c, :], lambda c: ("xn", c))

    def lru_mixer(i, bi, ctx, post, mask_state=False, before_out=None):
        j = i // 2
        xb = xbuf[bi]
        s_in = use_piece((i, "win"))
        prev_pair = []
        for oc0 in range(0, NCH, 2):
            pair = []
            for oc in (oc0, oc0 + 1):
                b = next_bank()
                mm_group(b, lambda kc, oc=oc: wsl[:, s_in, kc * 1024 + oc * 128: kc * 1024 + (oc + 1) * 128],
                         lambda kc: xn[:, kc, :], NCH, lambda kc: [("wslot", s_in), ("xn", kc)])
                xi = rot("xr")
                pair.append((oc, xi))
                A("pool", (lambda e, oc=oc, xi=xi: e.tensor_copy(out=xr[:, xi, 0:3], in_=chalo[:, j, oc, 0:3])),
                  reads=[("chalo", j, oc)], writes=[("xr", xi)])
                A("act", (lambda e, oc=oc, xi=xi, b=b: e.activation(out=xr[:, xi, 3:3 + T], in_=ps[:, b, :], func=AF.Identity,
                                                                     bias=V(("b_in", j), oc), scale=1.0)),
                  reads=[("psum", b)], writes=[("xr", xi)])
                if mask_state:
                    A("pool", (lambda e, oc=oc, xi=xi: e.tensor_scalar(out=chalo[:, j, oc, 0:3], in0=xr[:, xi, T:T + 3], scalar1=flg[:, 0:1], scalar2=None, op0=ALU.mult)),
                      reads=[("xr", xi)], writes=[("chalo", j, oc)])
                else:
                    A("pool", (lambda e, oc=oc, xi=xi: e.tensor_copy(out=chalo[:, j, oc, 0:3], in_=xr[:, xi, T:T + 3])),
                      reads=[("xr", xi)], writes=[("chalo", j, oc)])
            for oc, xi in pair:
                A("dve", (lambda e, oc=oc, xi=xi: e.tensor_scalar(out=big[:, oc, 0:T], in0=xr[:, xi, 3:3 + T],
                                                                   scalar1=V(("cw3", j), oc), scalar2=V(("conv_b", j), oc),
                                                                   op0=ALU.mult, op1=ALU.add)),
                  reads=[("xr", xi)], writes=[("big", oc)])
            for k in (2, 1, 0):
                for oc, xi in pair:
                    A("dve", (lambda e, oc=oc, xi=xi, k=k: e.scalar_tensor_tensor(
                        out=big[:, oc, 0:T], in0=xr[:, xi, k:k + T], scalar=V(("cw%d" % k, j), oc), in1=big[:, oc, 0:T],
                        op0=ALU.mult, op1=ALU.add)),
                      reads=[("xr", xi), ("big", oc)], writes=[("big", oc)])
            for oc_ in prev_pair:
                A("pool", (lambda e, oc=oc_: e.tensor_copy(out=cvb[:, oc, :], in_=big[:, oc, 0:T])),
                  reads=[("big", oc_)], writes=[("cvb", oc_)])
            prev_pair = [oc for oc, xi in pair]
        for oc_ in prev_pair:
            A("pool", (lambda e, oc=oc_: e.tensor_copy(out=cvb[:, oc, :], in_=big[:, oc, 0:T])),
              reads=[("big", oc_)], writes=[("cvb", oc_)])
        s_y = use_piece((i, "wy"))
        for oc in range(NCH):
            b = next_bank()
            mm_group(b, lambda kc, oc=oc: wsl[:, s_y, kc * 1024 + oc * 128: kc * 1024 + (oc + 1) * 128],
                     lambda kc: xn[:, kc, :], NCH, lambda kc: [("wslot", s_y), ("xn", kc)])
            A("act", (lambda e, oc=oc, b=b: e.activation(out=gate[:, oc, :], in_=ps[:, b, :], func=AF.Gelu_apprx_tanh,
                                                          bias=V(("b_y", j), oc), scale=1.0)),
              reads=[("psum", b)], writes=[("gate", oc)])
        for oc in range(NCH):
            A("pool", (lambda e, oc=oc: e.tensor_scalar(out=xb[:, oc, :], in0=xb[:, oc, :], scalar1=Dc(dlru(j, 4), oc), scalar2=None, op0=ALU.add)),
              reads=[("x", bi, oc)], writes=[("x", bi, oc)])
        s_ax = use_piece((i, "wax"), 4096)

        def gb(name, hd, q):
            if hd % 2 == 0:
                ap = {"tha": g_tha[:, q, 0:T], "thx": g_thx[:, q, 0:T], "a": g_a[:, q, :], "m": g_m[:, q, :]}[name]
                return ap, [({"tha": "g_tha", "thx": "g_thx", "a": "g_a", "m": "g_m"}[name], q)]
            idx = {"tha": 0, "thx": 1, "a": 2, "m": 3}[name] * 2 + q
            ap = r2[:, 2 * idx:2 * idx + 2, :].rearrange("p a t -> p (a t)").bitcast(F32)
            return ap, [("r2", 2 * idx), ("r2", 2 * idx + 1)]

        for hd in range(4):
            B = {(nm, q): gb(nm, hd, q) for nm in ("tha", "thx", "a", "m") for q in range(2)}
            for q in range(2):
                oc = 2 * hd + q
                ba = next_bank()
                bx = next_bank()
                mm_group(ba, lambda kc, q=q, hd=hd: wsl[:, s_ax, (hd * 2 + kc) * 256 + q * 128:(hd * 2 + kc) * 256 + (q + 1) * 128],
                         lambda kc, hd=hd: cvb[:, 2 * hd + kc, :], 2, lambda kc, hd=hd: [("wslot", s_ax), ("cvb", 2 * hd + kc)])
                mm_group(bx, lambda kc, q=q, hd=hd: wsl[:, s_ax, 2048 + (hd * 2 + kc) * 256 + q * 128:2048 + (hd * 2 + kc) * 256 + (q + 1) * 128],
                         lambda kc, hd=hd: cvb[:, 2 * hd + kc, :], 2, lambda kc, hd=hd: [("wslot", s_ax), ("cvb", 2 * hd + kc)])
                (tha, ktha), (thx, kthx) = B[("tha", q)], B[("thx", q)]
                A("act", (lambda e, oc=oc, ba=ba, tha=tha: e.activation(out=tha, in_=ps[:, ba, :], func=AF.Tanh,
                                                                         scale=0.5, bias=Dc(dlru(j, 0), oc))),
                  reads=[("psum", ba)], writes=ktha)
                A("act", (lambda e, oc=oc, bx=bx, thx=thx: e.activation(out=thx, in_=ps[:, bx, :], func=AF.Tanh,
                                                                         scale=0.5, bias=Dc(dlru(j, 1), oc))),
                  reads=[("psum", bx)], writes=kthx)
            for q in range(2):
                oc = 2 * hd + q
                (tha, ktha), (ga, ka), (gm, km) = B[("tha", q)], B[("a", q)], B[("m", q)]
                A("act", (lambda e, oc=oc, tha=tha, ga=ga: e.activation(out=ga, in_=tha, func=AF.Exp,
                                                                         scale=Dc(dlru(j, 2), oc), bias=Dc(dlru(j, 2), oc))),
                  reads=ktha, writes=ka)
                A("act", (lambda e, oc=oc, tha=tha, gm=gm: e.activation(out=gm, in_=tha, func=AF.Exp,
                                                                         scale=Dc(dlru(j, 3), oc), bias=Dc(dlru(j, 3), oc))),
                  reads=ktha, writes=km)
            for q in range(2):
                gm, km = B[("m", q)]
                A("act", (lambda e, gm=gm: e.activation(out=gm, in_=gm, func=AF.Sqrt, scale=-0.25, bias=kcol[:, 1:2])),
                  reads=km, writes=km)
            for q in range(2):
                oc = 2 * hd + q
                thx, kthx = B[("thx", q)]
                A("dve", (lambda e, oc=oc, thx=thx: e.scalar_tensor_tensor(out=thx, in0=thx, scalar=1.0,
                                                                            in1=big[:, oc, 0:T], op0=ALU.add, op1=ALU.mult)),
                  reads=kthx + [("big", oc)], writes=kthx)
            for q in range(2):
                (thx, kthx), (gm, km) = B[("thx", q)], B[("m", q)]
                A("dve", (lambda e, thx=thx, gm=gm: e.tensor_tensor(out=thx, in0=thx, in1=gm, op=ALU.mult)),
                  reads=kthx + km, writes=kthx)
            for q in range(2):
                oc = 2 * hd + q
                (thx, kthx), (gm, km), (ga, ka) = B[("thx", q)], B[("m", q)], B[("a", q)]
                A("dve", (lambda e, oc=oc, thx=thx, gm=gm, ga=ga: e.tensor_tensor_scan(out=gm, data0=ga, data1=thx,
                                                                                       initial=hst[:, j, oc:oc + 1], op0=ALU.mult, op1=ALU.add)),
                  reads=ka + kthx + [("hst", j, oc)], writes=km)
            for q in range(2):
                oc = 2 * hd + q
                gm, km = B[("m", q)]
                if mask_state:
                    A("pool", (lambda e, oc=oc, gm=gm: e.tensor_scalar(out=hst[:, j, oc:oc + 1], in0=gm[:, T - 1:T], scalar1=flg[:, 0:1], scalar2=None, op0=ALU.mult)),
                      reads=km, writes=[("hst", j, oc)])
                else:
                    A("pool", (lambda e, oc=oc, gm=gm: e.tensor_copy(out=hst[:, j, oc:oc + 1], in_=gm[:, T - 1:T])),
                      reads=km, writes=[("hst", j, oc)])
                A("dve", (lambda e, oc=oc, gm=gm: e.tensor_tensor(out=xn[:, oc, :], in0=gm, in1=gate[:, oc, :], op=ALU.mult)),
                  reads=km + [("gate", oc)], writes=[("xn", oc)])
        if before_out is not None:
            before_out()
        s_o = use_piece((i, "wout"))
        for oc in range(NCH):
            b = next_bank()
            mm_group(b, lambda kc, oc=oc: wsl[:, s_o, kc * 1024 + oc * 128: kc * 1024 + (oc + 1) * 128],
                     lambda kc: xn[:, kc, :], NCH, lambda kc: [("wslot", s_o), ("xn", kc)])
            A("dve", (lambda e, oc=oc, b=b: e.scalar_tensor_tensor(out=xb[:, oc, :], in0=ps[:, b, :], scalar=Dc(dmod(i, 2), oc),
                                                                    in1=xb[:, oc, :], op0=ALU.mult, op1=ALU.add)),
              reads=[("psum", b), ("x", bi, oc)], writes=[("x", bi, oc)])
            post(oc)

    def pool_mixer(i, bi, ctx, post, tabw=None, mask_state=False):
        first = tabw is not None
        j = i // 2
        xb = xbuf[bi]
        for c in range(NCH):
            A("pool", (lambda e, c=c: e.tensor_copy(out=big[:, c, 0:HALO], in_=vhalo[:, j, c, :])),
              reads=[("vhalo", j, c)], writes=[("big", c)])
        norm_apply(ctx, bi, lambda c: Dc(dmod(i, 6), c), zero_b,
                   lambda c: big[:, c, HALO:HALO + T], lambda c: ("big", c))
        for c in range(NCH):
            if mask_state:
                A("pool", (lambda e, c=c: e.tensor_scalar(out=vhalo[:, j, c, :], in0=big[:, c, T:T + HALO], scalar1=flg[:, 0:1], scalar2=None, op0=ALU.mult)),
                  reads=[("big", c)], writes=[("vhalo", j, c)])
            else:
                A("pool", (lambda e, c=c: e.tensor_copy(out=vhalo[:, j, c, :], in_=big[:, c, T:T + HALO])),
                  reads=[("big", c)], writes=[("vhalo", j, c)])
        for c0 in range(0, NCH, 2):
            g = c0 // 2
            win = 2 << g
            st = []
            for w_, c in enumerate((c0, c0 + 1)):
                st.append({"c": c, "src": big[:, c, :], "skey": ("big", c), "buf": (pl, "g_tha") if w_ == 0 else (plb, "g_thx"), "ti": w_})
            for l in range(1, g + 2):
                sh = 1 << (l - 1)
                lo = 1 << l
                di = l % 2
                for d in st:
                    dst = d["buf"][0][:, di, :]
                    dkey = (d["buf"][1], di)
                    A("dve", (lambda e, src=d["src"], dst=dst, lo=lo, sh=sh: e.tensor_tensor(
                        out=dst[:, lo:HALO + T], in0=src[:, lo:HALO + T], in1=src[:, lo - sh:HALO + T - sh], op=ALU.add)),
                      reads=[d["skey"]], writes=[dkey])
                    d["src"], d["skey"] = dst, dkey
            for d in st:
                A("dve", (lambda e, c=d["c"], src=d["src"], win=win: e.scalar_tensor_tensor(
                    out=xn[:, c, :], in0=src[:, HALO:HALO + T], scalar=1.0 / win, in1=big[:, c, HALO:HALO + T],
                    op0=ALU.mult, op1=ALU.subtract)),
                  reads=[d["skey"], ("big", d["c"])], writes=[("xn", d["c"])])
            if first:
                for d in st:
                    A("dve", (lambda e, g=g, src=d["src"], ti=d["ti"]: e.tensor_tensor(out=tn[:, ti, 0:HALO], in0=src[:, HALO:2 * HALO],
                                                                                   in1=tabs[:, tabw, g, :], op=ALU.mult)),
                      reads=[d["skey"]], writes=[("tn", d["ti"])])
                for d in st:
                    A("dve", (lambda e, c=d["c"], ti=d["ti"]: e.tensor_tensor(out=xn[:, c, 0:HALO], in0=tn[:, ti, 0:HALO],
                                                                           in1=big[:, c, HALO:2 * HALO], op=ALU.subtract)),
                      reads=[("tn", d["ti"]), ("big", d["c"])], writes=[("xn", d["c"])])
        s_p = use_piece((i, "pw"), 2048)
        for oc in range(NCH):
            g, q = oc // 2, oc % 2
            b = next_bank()
            mm_group(b, lambda kc, g=g, q=q: wsl[:, s_p, (g * 2 + kc) * 256 + q * 128:(g * 2 + kc) * 256 + (q + 1) * 128],
                     lambda kc, g=g: xn[:, 2 * g + kc, :], 2, lambda kc, g=g: [("wslot", s_p), ("xn", 2 * g + kc)])
            A("dve", (lambda e, oc=oc, b=b: e.scalar_tensor_tensor(out=xb[:, oc, :], in0=ps[:, b, :], scalar=Dc(dpool(j), oc),
                                                                    in1=xb[:, oc, :], op0=ALU.mult, op1=ALU.add)),
              reads=[("psum", b), ("x", bi, oc)], writes=[("x", bi, oc)])
            post(oc)

    def ffn(i, bi, ctx, post, mid=None):
        xb = xbuf[bi]
        norm_apply(ctx, bi, lambda c: Dc(dmod(i, 7), c), lambda c: Dc(dmod(i, 3), c),
                   lambda c: xn[:, c, :], lambda c: ("xn", c))
        for pj in range(4):
            s = use_piece((i, "w1", pj))
            for q in range(8):
                hc = pj * 8 + q
                b = next_bank()
                mm_group(b, lambda kc, q=q, s=s: wsl[:, s, kc * 1024 + q * 128: kc * 1024 + (q + 1) * 128],
                         lambda kc: xn[:, kc, :], NCH, lambda kc, s=s: [("wslot", s), ("xn", kc)])
                ti = rot("tn")
                A("act", (lambda e, ti=ti, b=b: e.activation(out=tn[:, ti, :], in_=ps[:, b, :], func=AF.Relu)),
                  reads=[("psum", b)], writes=[("tn", ti)])
                eng = "dve" if hc % 2 == 0 else "pool"
                A(eng, (lambda e, ti=ti, hc=hc: e.tensor_tensor(out=r2[:, hc, :], in0=tn[:, ti, :], in1=tn[:, ti, :], op=ALU.mult)),
                  reads=[("tn", ti)], writes=[("r2", hc)])
        if mid is not None:
            mid()
        for pj in range(4):
            s = use_piece((i, "w2", pj))
            for q in range(2):
                oc = pj * 2 + q
                b = next_bank()
                mm_group(b, lambda hc, q=q, s=s: wsl[:, s, hc * 256 + q * 128: hc * 256 + (q + 1) * 128],
                         lambda hc: r2[:, hc, :], HC, lambda hc, s=s: [("wslot", s), ("r2", hc)])
                A("dve", (lambda e, oc=oc, b=b: e.scalar_tensor_tensor(out=xb[:, oc, :], in0=ps[:, b, :], scalar=Dc(dmod(i, 5), oc),
                                                                        in1=xb[:, oc, :], op0=ALU.mult, op1=ALU.add)),
                  reads=[("psum", b), ("x", bi, oc)], writes=[("x", bi, oc)])
                post(oc)

    def load_x(p):
        bi = p % 2
        if p < LAG:
            A("act", (lambda e: e.dma_start(out=xbuf[bi][:], in_=xT3[:, :, p * T:(p + 1) * T])),
              writes=[("x", bi, c) for c in range(NCH)], dma=("xin", bi))
        else:
            k = p - LAG
            A("act", (lambda e: e.dma_start(out=xbuf[bi][:], in_=cc_out[k].rearrange("p (c t) -> p c t", c=NCH))),
              reads=[("ccout", k)], writes=[("x", bi, c) for c in range(NCH)], dma=("xin", bi))

    def final(bi, jt, ctx):
        ost = lambda c: r2[:, 2 * c:2 * c + 2, :].rearrange("p a t -> p (a t)").bitcast(F32)
        norm_apply(ctx, bi, lambda c: V(("fg",), c), zero_b, ost, lambda c: ("r2", 2 * c))
        A("act", (lambda e: e.dma_start(out=oT3[:, :, jt * T:(jt + 1) * T],
                                        in_=r2[:, 0:2 * NCH, :].rearrange("p a t -> p (a t)").bitcast(F32).rearrange("p (c t) -> p c t", c=NCH))),
          reads=[("r2", q) for q in range(2 * NCH)], writes=[("out", jt)], dma=("out",))

    def send(bi, k):
        xb = xbuf[bi]
        for c in range(NCH):
            A("act", (lambda e, c=c: e.activation(out=xb[:, c, :], in_=xb[:, c, :], func=AF.Identity, scale=flg[:, 0:1], bias=kcol[:, 3:4])),
              reads=[("x", bi, c)], writes=[("x", bi, c)])
        A("act", (lambda e: e.dma_start(out=cc_in[k][128:256, :].rearrange("p (c t) -> p c t", c=NCH), in_=xb[:])),
          reads=[("x", bi, c) for c in range(NCH)], writes=[("ccin1", k)], dma=("send", k % 2))
        A("pool", (lambda e: e.collective_compute("ReduceScatter", ALU.add, replica_groups=GROUPS,
                                                  ins=[cc_in[k].opt()], outs=[cc_out[k].opt()])),
          reads=[("ccin0",), ("ccin1", k)], writes=[("ccout", k)], dma="cc", inc=1)

    sch.extra_reads = (("der",), ("vecs",), ("kcol",), ("ones",), ("invc",), ("hst",), ("chalo",), ("vhalo",), ("tabs",), ("flg",))
    load_x(0)
    if LAG > 1:
        load_x(1)
    ctx0 = stat_begin()
    for c in range(NCH):
        stat_chunk(ctx0, 0, c)
    stat_finish(ctx0)
    nothing = lambda oc: None
    deferred = None
    lru_norm(0, 0, ctx0)
    for p in range(NP):
        bi = p % 2
        ms = p < LAG
        if 1 < p + 1 < LAG:
            load_x(p + 1)
        ctxA = stat_begin()
        lru_mixer(0, bi, ctx0, (lambda oc, ctxA=ctxA: stat_chunk(ctxA, bi, oc)), mask_state=ms, before_out=deferred)
        deferred = None
        stat_finish(ctxA)
        ctxB = stat_begin()
        ffn(0, bi, ctxA, (lambda oc, ctxB=ctxB: stat_chunk(ctxB, bi, oc)))
        stat_finish(ctxB)
        if p + 1 < NP and p + 1 >= LAG:
            load_x(p + 1)
        ctxC = stat_begin()
        pool_mixer(1, bi, ctxB, (lambda oc, ctxC=ctxC: stat_chunk(ctxC, bi, oc)),
                   tabw=(0 if p == 0 else (1 if p == LAG else None)), mask_state=ms)
        stat_finish(ctxC)
        nxt = {}

        def mid(p=p, nxt=nxt):
            if p + 1 < NP:
                c0 = stat_begin()
                for c in range(NCH):
                    stat_chunk(c0, (p + 1) % 2, c)
                nxt["ctx"] = c0
        do_final = p >= LAG
        ctxF = {}

        def mid2(p=p):
            mid()
            if do_final:
                ctxF["ctx"] = stat_begin()
        ffn(1, bi, ctxC, (lambda oc: stat_chunk(ctxF["ctx"], bi, oc)) if do_final else nothing, mid=mid2)
        if "ctx" in nxt:
            stat_finish(nxt["ctx"])
            lru_norm(0, (p + 1) % 2, nxt["ctx"])
        if do_final:
            stat_finish(ctxF["ctx"])

        def deferred(p=p, bi=bi, do_final=do_final, cf=ctxF.get("ctx")):
            if do_final:
                final(bi, p - LAG, cf)
            if p < NCC:
                send(bi, p)
        ctx0 = nxt.get("ctx")
    if deferred is not None:
        deferred()

    if debug:
        dbg_der = nc.dram_tensor("dbg_der", [128, ND * NCH], F32, kind="ExternalOutput").ap()
        dbg_gate = nc.dram_tensor("dbg_gate", [128, NCH * T], BF16, kind="ExternalOutput").ap()
        dbg_cvb = nc.dram_tensor("dbg_cvb", [128, NCH * T], BF16, kind="ExternalOutput").ap()
        dbg_r2 = nc.dram_tensor("dbg_r2", [128, HC * T], BF16, kind="ExternalOutput").ap()
        dbg_x = nc.dram_tensor("dbg_x", [128, NCH * T], F32, kind="ExternalOutput").ap()
        A("act", (lambda e: e.dma_start(out=dbg_der, in_=der[:].rearrange("p a b -> p (a b)"))), reads=[("der",)], dma=("dbg", 0))
        A("act", (lambda e: e.dma_start(out=dbg_gate, in_=gate[:].rearrange("p a b -> p (a b)"))), reads=[("gate", c) for c in range(NCH)], dma=("dbg", 1))
        A("act", (lambda e: e.dma_start(out=dbg_cvb, in_=cvb[:].rearrange("p a b -> p (a b)"))), reads=[("cvb", c) for c in range(NCH)], dma=("dbg", 2))
        A("act", (lambda e: e.dma_start(out=dbg_r2, in_=r2[:].rearrange("p a b -> p (a b)"))), reads=[("r2", c) for c in range(HC)], dma=("dbg", 3))
        A("act", (lambda e: e.dma_start(out=dbg_x, in_=xbuf[(NP - 1) % 2][:].rearrange("p a b -> p (a b)"))), reads=[("x", (NP - 1) % 2, c) for c in range(NCH)], dma=("dbg", 4))
    sch.assign(nc, stack)
    with nc.Block() as block:
        @block.tensor
        def _(e):
            sch.emit_engine("pe", e)

        @block.vector
        def _(e):
            sch.emit_engine("dve", e)

        @block.gpsimd
        def _(e):
            sch.emit_engine("pool", e)

        @block.sync
        def _(e):
            sch.emit_engine("sp", e)

        @block.scalar
        def _(e):
            sch.emit_engine("act", e, tail_waits=sch.dma_final)
    stack.close()
    return nc


def _pk(v):
    return np.ascontiguousarray(np.asarray(v, np.float32).reshape(NCH, 128).T)


def prep_inputs(inputs, NT=SEQ // T):
    f = lambda k: np.asarray(inputs[k], np.float32)
    bm = f("b_mod")
    x = f("x")
    c = f("c")
    stage_shared = []
    for s in range(2):
        vecs = np.zeros((128, NV, NCH), np.float32)
        for il in range(2):
            i = 2 * s + il
            vecs[:, VIDX[("gmix", il)]] = _pk(f("norm_mix_g")[i])
            vecs[:, VIDX[("gffn", il)]] = _pk(f("norm_ffn_g")[i])
            for k in range(6):
                vecs[:, VIDX[("bmod", il, k)]] = _pk(bm[i, k * D:(k + 1) * D])
        for nm, key in (("b_y", "lru_b_y"), ("b_in", "lru_b_in"), ("conv_b", "lru_conv_b"), ("b_a", "lru_b_a"),
                        ("b_x", "lru_b_x"), ("lam", "lru_lambda"), ("b_out", "lru_b_out")):
            vecs[:, VIDX[(nm, 0)]] = _pk(f(key)[s].reshape(-1))
        for k in range(4):
            vecs[:, VIDX[("cw%d" % k, 0)]] = _pk(f("lru_conv_w")[s, k])
        vecs[:, VIDX[("pscale", 0)]] = _pk(f("pool_scale")[s])
        vecs[:, VIDX[("fg",)]] = _pk(f("final_norm_g"))
        flags = np.zeros((128, 4), np.float32)
        flags[:, s] = 1.0
        stage_shared.append({
            "vecs": np.ascontiguousarray(vecs.reshape(128, NV * NCH)),
            "flags": flags,
            "w_mod": np.ascontiguousarray(f("w_mod")[2 * s:2 * s + 2]),
            "lru_w_y": np.ascontiguousarray(f("lru_w_y")[s:s + 1]),
            "lru_w_in": np.ascontiguousarray(f("lru_w_in")[s:s + 1]),
            "lru_w_out": np.ascontiguousarray(f("lru_w_out")[s:s + 1]),
            "lru_w_a": np.ascontiguousarray(f("lru_w_a")[s:s + 1]),
            "lru_w_x": np.ascontiguousarray(f("lru_w_x")[s:s + 1]),
            "pool_w": np.ascontiguousarray(f("pool_w")[s:s + 1]),
            "ffn_w1": np.ascontiguousarray(f("ffn_w1")[2 * s:2 * s + 2]),
            "ffn_w2": np.ascontiguousarray(f("ffn_w2")[2 * s:2 * s + 2]),
        })
    zeros_xT = np.zeros((D, NT * T), np.float32)
    maps = []
    for b in range(x.shape[0]):
        for s in range(2):
            m = dict(stage_shared[s])
            m["xT"] = np.ascontiguousarray(x[b, :NT * T].T) if s == 0 else zeros_xT
            m["cvec"] = _pk(c[b])
            maps.append(m)
    return maps


_NC_CACHE = {}


def kernel(**inputs):
    NT = SEQ // T
    maps = prep_inputs(inputs, NT)
    if NT not in _NC_CACHE:
        _NC_CACHE[NT] = build_program(NT)
    nc = _NC_CACHE[NT]
    res = run_bass_kernel_spmd(nc, maps, core_ids=list(range(8)))
    out = np.stack([np.ascontiguousarray(res.results[2 * b + 1]["outT"].T) for b in range(BATCH)])
    return out.astype(np.float32)
```

```python
import numpy as np
from contextlib import ExitStack
import concourse.bass as bass
import concourse.mybir as mybir
from concourse.bass_utils import run_bass_kernel_spmd

F32 = mybir.dt.float32
BF16 = mybir.dt.bfloat16
AF = mybir.ActivationFunctionType
ALU = mybir.AluOpType

D = 1024
NCH = 8
FF = 4096
HC = 32
T = 512
DEPTH = 4
SEQ = 8192
BATCH = 4
EPS = 1e-6
NSLOT = 3
SLOTW = 8192
HALO = 16
SEM_CHUNK = 6000
SAME_ENGINE_FULL = True


def vec_index():
    names = []
    for i in range(DEPTH):
        names += [("gmix", i), ("gffn", i)] + [("bmod", i, k) for k in range(6)]
    for j in range(2):
        names += [(n, j) for n in ("b_y", "b_in", "cw0", "cw1", "cw2", "cw3", "conv_b",
                                   "b_a", "b_x", "lam", "b_out")]
    for j in range(2):
        names.append(("pscale", j))
    names.append(("fg",))
    return {n: k for k, n in enumerate(names)}


VIDX = vec_index()
NV = len(VIDX)


class Op:
    __slots__ = ("eng", "emit", "deps", "dma", "wkeys", "signal", "sem", "ticket", "inc")


class Sched:
    ENG = ("pe", "act", "dve", "pool", "sp")

    def __init__(self):
        self.q = {e: [] for e in self.ENG}
        self.lastw = {}
        self.readers = {}
        self.extra_reads = ()

    def add(self, eng, emit, reads=(), writes=(), dma=None, inc=16):
        op = Op()
        op.eng, op.emit, op.dma = eng, emit, dma
        op.inc = inc
        reads = tuple(reads) + tuple(self.extra_reads)
        op.wkeys = set(writes)
        op.signal = dma is not None
        op.sem = None
        op.ticket = None
        rset = set(reads)
        deps = set()
        for k in reads:
            w = self.lastw.get(k)
            if w is not None:
                deps.add(w)
        for k in writes:
            w = self.lastw.get(k)
            if w is not None:
                deps.add(w)
            for r in self.readers.get(k, ()):
                deps.add(r)
        fdeps = []
        for d in deps:
            if d.eng == eng and d.dma is None and dma is None:
                if eng == "pe":
                    continue
                if not SAME_ENGINE_FULL and not (d.wkeys & rset):
                    continue
            d.signal = True
            fdeps.append(d)
        op.deps = fdeps
        for k in reads:
            self.readers.setdefault(k, []).append(op)
        for k in writes:
            self.lastw[k] = op
            self.readers[k] = []
        self.q[eng].append(op)
        return op

    def assign(self, nc, stack):
        self.finals = []
        dma_sems = {}
        dma_cnt = {}
        for e in self.ENG:
            cnt = 0
            sem = None
            for op in self.q[e]:
                if op.dma is not None:
                    if op.dma not in dma_sems:
                        dma_sems[op.dma] = stack.enter_context(nc.semaphore("d_%s" % str(op.dma)))
                        dma_cnt[op.dma] = 0
                    dma_cnt[op.dma] += op.inc
                    op.sem, op.ticket = dma_sems[op.dma], dma_cnt[op.dma]
                elif op.signal:
                    if sem is None or cnt >= SEM_CHUNK:
                        sem = stack.enter_context(nc.semaphore("e_%s_%d" % (e, len(self.finals))))
                        self.finals.append(sem)
                        cnt = 0
                    cnt += 1
                    op.sem, op.ticket, op.inc = sem, cnt, 1
        self.dma_final = [(dma_sems[k], dma_cnt[k]) for k in dma_sems]

    def emit_engine(self, e, engobj, tail_waits=()):
        waited = {}
        for op in self.q[e]:
            need = {}
            for d in op.deps:
                if need.get(d.sem, (None, 0))[1] < d.ticket:
                    need[d.sem] = (d.sem, d.ticket)
            for sem, t in need.values():
                if waited.get(sem, 0) < t:
                    engobj.wait_ge(sem, t)
                    waited[sem] = t
            ins = op.emit(engobj)
            if op.sem is not None:
                if op.dma is not None and op.inc == 1:
                    ins.then_inc(op.sem)
                else:
                    ins.then_inc(op.sem, op.inc)
        for sem, t in tail_waits:
            if waited.get(sem, 0) < t:
                engobj.wait_ge(sem, t)


def build_program(NT, nlayers=2, debug=False, LAG=2):
    S = NT * T
    NL = nlayers
    NP = NT + LAG
    nc = bass.Bass("TRN2", target_bir_lowering=False)
    sch = Sched()
    stack = ExitStack()

    def din(name, shape, dt=F32):
        return nc.dram_tensor(name, list(shape), dt, kind="ExternalInput").ap()

    xT = din("xT", [D, S])
    cvec = din("cvec", [128, NCH])
    vecs_d = din("vecs", [128, NV * NCH])
    w_mod = din("w_mod", [NL, D, 6 * D])
    flags_d = din("flags", [128, 4])
    w_y = din("lru_w_y", [1, D, D])
    w_in = din("lru_w_in", [1, D, D])
    w_out = din("lru_w_out", [1, D, D])
    w_a = din("lru_w_a", [1, 4, 256, 256])
    w_x = din("lru_w_x", [1, 4, 256, 256])
    pool_w = din("pool_w", [1, 4, 256, 256])
    w1 = din("ffn_w1", [NL, D, FF])
    w2 = din("ffn_w2", [NL, FF, D])
    outT = nc.dram_tensor("outT", [D, S], F32, kind="ExternalOutput").ap()
    NCC = NP - LAG
    cc_in = [nc.dram_tensor("cc_in%d" % k, [256, NCH * T], F32, kind="Internal").ap() for k in range(NCC)]
    cc_out = [nc.dram_tensor("cc_out%d" % k, [128, NCH * T], F32, kind="Internal").ap() for k in range(NCC)]
    GROUPS = [[0, 1], [2, 3], [4, 5], [6, 7]]

    def sb(name, shape, dt):
        return stack.enter_context(nc.sbuf_tensor(name, list(shape), dt))

    wsl = sb("wsl", [128, NSLOT, SLOTW], BF16)
    xbuf = [sb("xbuf%d" % i, [128, NCH, T], F32) for i in range(2)]
    xn = sb("xn", [128, NCH, T], BF16)
    cvb = sb("cvb", [128, NCH, T], BF16)
    gate = sb("gate", [128, NCH, T], BF16)
    big = sb("big", [128, NCH, HALO + T], F32)
    r2 = sb("r2", [128, HC, T], BF16)
    NSQ = 10
    sq = sb("sq", [128, NSQ, T], BF16)
    stdt = sb("stdt", [128, 2, T], F32)
    rstd = sb("rstd", [128, 2, T], F32)

    tn = sb("tn", [128, 2, T], F32)
    xr = sb("xr", [128, 4, 4 + T], F32)
    g_tha = sb("g_tha", [128, 2, HALO + T], F32)
    g_thx = sb("g_thx", [128, 2, HALO + T], F32)
    g_a = sb("g_a", [128, 2, T], F32)
    g_m = sb("g_m", [128, 2, T], F32)
    pl = g_tha
    plb = g_thx
    vecs = sb("vecs_s", [128, NV, NCH], F32)
    ND = DEPTH * 8 + 2 * 5 + 2
    der = sb("der", [128, ND, NCH], F32)
    cvt = sb("cvt", [128, NCH], F32)
    cond = sb("cond", [128, NCH], F32)
    ctmp = sb("ctmp", [128, NCH], F32)
    kcol = sb("kcol", [128, 4], F32)
    ones = sb("ones", [128, 128], BF16)
    hst = sb("hst", [128, 2, NCH], F32)
    chalo = sb("chalo", [128, 2, NCH, 4], F32)
    vhalo = sb("vhalo", [128, 2, NCH, HALO], F32)
    invc = sb("invc", [128, 4, HALO], F32)
    tabs = sb("tabs", [128, 2, 4, HALO], F32)
    flg = sb("flg", [128, 4], F32)
    ps = stack.enter_context(nc.psum_tensor("ps", [128, 8, T], F32))

    def dmod(i, part):
        return i * 8 + part
    def dlru(j, k):
        return DEPTH * 8 + j * 5 + k
    def dpool(j):
        return DEPTH * 8 + 10 + j

    def V(name, c):
        return vecs[:, VIDX[name], c:c + 1]
    def Dc(idx, c):
        return der[:, idx, c:c + 1]

    bank_ctr = [0]
    def next_bank():
        b = bank_ctr[0] % 6
        bank_ctr[0] += 1
        return b

    scr = {}
    def scr_piece(key):
        if key not in scr:
            scr[key] = nc.dram_tensor("scr_%d" % len(scr), [128, SLOTW], BF16, kind="Internal").ap()
        return scr[key]

    A = sch.add
    A("sp", lambda e: e.dma_start(out=vecs[:].rearrange("p a b -> p (a b)"), in_=vecs_d), writes=[("vecs",)], dma="vecs")
    A("sp", lambda e: e.dma_start(out=cvt[:], in_=cvec), writes=[("cvt",)], dma="cvt")
    A("pool", lambda e: e.memset(kcol[:, 0:1], EPS), writes=[("kcol",)])
    A("pool", lambda e: e.memset(kcol[:, 1:2], 0.25 + 5e-7), writes=[("kcol",)])
    A("pool", lambda e: e.memset(kcol[:, 2:3], 1.0), writes=[("kcol",)])
    A("pool", lambda e: e.memset(kcol[:, 3:4], 0.0), writes=[("kcol",)])
    A("pool", lambda e: e.memset(ones[:], 1.0 / D), writes=[("ones",)])
    A("pool", lambda e: e.memset(hst[:], 0.0), writes=[("hst",)])
    A("pool", lambda e: e.memset(chalo[:], 0.0), writes=[("chalo",)])
    A("pool", lambda e: e.memset(vhalo[:], 0.0), writes=[("vhalo",)])
    for g in range(4):
        win = 2 << g
        for t in range(HALO):
            val = 1.0 / min(t + 1, win)
            A("pool", (lambda e, g=g, t=t, val=val: e.memset(invc[:, g, t:t + 1], val)), writes=[("invc",)])
    A("sp", lambda e: e.dma_start(out=flg[:], in_=flags_d), writes=[("flg",)], dma="flg")
    for w_ in range(2):
        for g in range(4):
            win = 2 << g
            A("dve", (lambda e, w_=w_, g=g, win=win: e.tensor_scalar(out=tabs[:, w_, g, :], in0=invc[:, g, :], scalar1=-1.0 / win, scalar2=None, op0=ALU.add)),
              reads=[("invc",)], writes=[("tabs",)])
            A("dve", (lambda e, w_=w_, g=g: e.tensor_scalar(out=tabs[:, w_, g, :], in0=tabs[:, w_, g, :], scalar1=flg[:, w_:w_ + 1], scalar2=None, op0=ALU.mult)),
              reads=[("tabs",), ("flg",)], writes=[("tabs",)])
            A("dve", (lambda e, w_=w_, g=g, win=win: e.tensor_scalar(out=tabs[:, w_, g, :], in0=tabs[:, w_, g, :], scalar1=1.0 / win, scalar2=None, op0=ALU.add)),
              reads=[("tabs",)], writes=[("tabs",)])
    xT3 = xT.rearrange("(k p) t -> p k t", p=128)
    oT3 = outT.rearrange("(k p) t -> p k t", p=128)
    for k in range(NCC):
        tsrc = min(k + LAG, NT - 1)
        A("sp", (lambda e, k=k, tsrc=tsrc: e.dma_start(out=cc_in[k][0:128, :].rearrange("p (c t) -> p c t", c=NCH),
                                                        in_=xT3[:, :, tsrc * T:(tsrc + 1) * T])),
          writes=[("ccin0",)], dma="ccin0")
    A("act", lambda e: e.activation(out=ctmp[:], in_=cvt[:], func=AF.Tanh, scale=0.5), reads=[("cvt",)], writes=[("ctmp",)])
    A("dve", lambda e: e.scalar_tensor_tensor(out=cond[:], in0=ctmp[:], scalar=1.0, in1=cvt[:], op0=ALU.add, op1=ALU.mult),
      reads=[("ctmp",), ("cvt",)], writes=[("cond",)])
    A("dve", lambda e: e.tensor_scalar(out=cond[:], in0=cond[:], scalar1=0.5, scalar2=None, op0=ALU.mult),
      reads=[("cond",)], writes=[("cond",)])

    slot_ctr = [0]
    def next_slot():
        s = slot_ctr[0] % NSLOT
        slot_ctr[0] += 1
        return s

    MODW = 512
    mod_tasks = []
    def mod_layer(i):
        def piece(pc, i=i):
            s = next_slot()
            stg = wsl[:, s, :].bitcast(F32)
            src = w_mod[i].rearrange("(k p) e -> p k e", p=128)[:, :, pc * MODW:(pc + 1) * MODW]
            A("sp", (lambda e, stg=stg, src=src: e.dma_start(out=stg.rearrange("p (k e) -> p k e", k=NCH), in_=src)),
              writes=[("wslot", s)], dma=("slot", s))
            for jj in range(MODW // 128):
                jcol = pc * (MODW // 128) + jj
                for kc in range(NCH):
                    A("pe", (lambda e, stg=stg, kc=kc, jj=jj, jcol=jcol: e.matmul(
                        ps[:, 7, jcol:jcol + 1], lhsT=stg[:, kc * MODW + jj * 128: kc * MODW + (jj + 1) * 128],
                        rhs=cond[:, kc:kc + 1], start=(kc == 0), stop=(kc == NCH - 1))),
                      reads=[("wslot", s), ("cond",)], writes=[("psum", 7)])
        for pc in range(6 * D // MODW):
            mod_tasks.append(lambda pc=pc, piece=piece: piece(pc))
        def rest(i=i):
            i0 = VIDX[("bmod", i, 0)]
            A("dve", (lambda e, i=i, i0=i0: e.tensor_tensor(
                out=der[:, dmod(i, 0):dmod(i, 0) + 6, :].rearrange("p a b -> p (a b)"), in0=ps[:, 7, 0:48],
                in1=vecs[:, i0:i0 + 6, :].rearrange("p a b -> p (a b)"), op=ALU.add)),
              reads=[("psum", 7), ("vecs",)], writes=[("der",)])
            A("dve", (lambda e, i=i: e.scalar_tensor_tensor(out=der[:, dmod(i, 6), :], in0=der[:, dmod(i, 1), :], scalar=1.0,
                                                             in1=vecs[:, VIDX[("gmix", i)], :], op0=ALU.add, op1=ALU.mult)),
              reads=[("der",), ("vecs",)], writes=[("der",)])
            A("dve", (lambda e, i=i: e.scalar_tensor_tensor(out=der[:, dmod(i, 7), :], in0=der[:, dmod(i, 4), :], scalar=1.0,
                                                             in1=vecs[:, VIDX[("gffn", i)], :], op0=ALU.add, op1=ALU.mult)),
              reads=[("der",), ("vecs",)], writes=[("der",)])
            j = i // 2
            if i % 2 == 0:
                A("dve", (lambda e, j=j: e.tensor_scalar(out=der[:, dlru(j, 0), :], in0=vecs[:, VIDX[("b_a", j)], :], scalar1=0.5, scalar2=None, op0=ALU.mult)),
                  reads=[("vecs",)], writes=[("der",)])
                A("dve", (lambda e, j=j: e.tensor_scalar(out=der[:, dlru(j, 1), :], in0=vecs[:, VIDX[("b_x", j)], :], scalar1=0.5, scalar2=None, op0=ALU.mult)),
                  reads=[("vecs",)], writes=[("der",)])
                A("act", (lambda e, j=j: e.activation(out=ctmp[:], in_=vecs[:, VIDX[("lam", j)], :], func=AF.Exp, scale=-1.0)),
                  reads=[("vecs",), ("ctmp",)], writes=[("ctmp",)])
                A("act", (lambda e: e.activation(out=ctmp[:], in_=ctmp[:], func=AF.Ln, bias=kcol[:, 2:3], scale=1.0)),
                  reads=[("ctmp",), ("kcol",)], writes=[("ctmp",)])
                A("dve", (lambda e, j=j: e.tensor_scalar(out=der[:, dlru(j, 2), :], in0=ctmp[:], scalar1=-4.0, scalar2=None, op0=ALU.mult)),
                  reads=[("ctmp",)], writes=[("der",)])
                A("dve", (lambda e, j=j: e.tensor_scalar(out=der[:, dlru(j, 3), :], in0=ctmp[:], scalar1=-8.0, scalar2=None, op0=ALU.mult)),
                  reads=[("ctmp",)], writes=[("der",), ("ctmp",)])
                A("dve", (lambda e, i=i, j=j: e.tensor_tensor(out=der[:, dlru(j, 4), :], in0=der[:, dmod(i, 2), :],
                                                               in1=vecs[:, VIDX[("b_out", j)], :], op=ALU.mult)),
                  reads=[("der",), ("vecs",)], writes=[("der",)])
            else:
                A("dve", (lambda e, i=i, j=j: e.tensor_tensor(out=der[:, dpool(j), :], in0=der[:, dmod(i, 2), :],
                                                               in1=vecs[:, VIDX[("pscale", j)], :], op=ALU.mult)),
                  reads=[("der",), ("vecs",)], writes=[("der",)])

        mod_tasks.append(rest)
    for i in range(nlayers):
        mod_layer(i)

    cast_rr = [0]
    ostg_ctr = [0]
    piece_sub = {}
    conv_tasks = []

    def convert(src, dst, shape3, pkey):
        sub = (pkey, len(piece_sub.setdefault(pkey, [])))
        piece_sub[pkey].append(sub)
        conv_tasks.append(lambda: convert_now(src, dst, shape3, sub))

    def convert_now(src, dst, shape3, sub):
        a, b = shape3
        n = a * b
        s = next_slot()
        stg = wsl[:, s, :].bitcast(F32)[:, 0:n]
        o = ostg_ctr[0] % 2
        ostg_ctr[0] += 1
        ost = r2[:, o * 8:(o + 1) * 8, :].rearrange("p a b -> p (a b)")[:, 0:n]
        okeys = [("r2", o * 8 + q) for q in range(8)]
        A("sp", (lambda e: e.dma_start(out=stg.rearrange("p (a b) -> p a b", a=a), in_=src)),
          writes=[("wslot", s)], dma=("slot", s))
        ce = ("dve", "act", "pool")[cast_rr[0] % 3]
        cast_rr[0] += 1
        if ce == "act":
            A("act", (lambda e: e.activation(out=ost, in_=stg, func=AF.Copy)), reads=[("wslot", s)], writes=okeys)
        else:
            A(ce, (lambda e: e.tensor_copy(out=ost, in_=stg)), reads=[("wslot", s)], writes=okeys)
        A("sp", (lambda e: e.dma_start(out=dst, in_=ost)), reads=okeys, writes=[("scr", sub)], dma=("ostg", o))

    def conv_dense(wap, key_fn, ncols_total):
        for pj in range(ncols_total // 1024):
            dstp = scr_piece(key_fn(pj))
            for half in range(2):
                src = wap.rearrange("(k p) n -> p k n", p=128)[:, half * 4:(half + 1) * 4, pj * 1024:(pj + 1) * 1024]
                convert(src, dstp[:, half * 4096:(half + 1) * 4096], (4, 1024), key_fn(pj))

    def conv_w2(wap, key_fn):
        for pj in range(4):
            dstp = scr_piece(key_fn(pj))
            for half in range(2):
                src = wap.rearrange("(k p) n -> p k n", p=128)[:, half * 16:(half + 1) * 16, pj * 256:(pj + 1) * 256]
                convert(src, dstp[:, half * 4096:(half + 1) * 4096], (16, 256), key_fn(pj))

    def conv_bd(wap, dst, pkey):
        src = wap.rearrange("h (k p) n -> p (h k) n", p=128)
        convert(src, dst, (8, 256), pkey)

    for i in range(nlayers):
        j = i // 2
        if i % 2 == 0:
            conv_dense(w_y[j], lambda pj, i=i: (i, "wy"), 1024)
            conv_dense(w_in[j], lambda pj, i=i: (i, "win"), 1024)
            pbd = scr_piece((i, "wax"))
            conv_bd(w_a[j], pbd[:, 0:2048], (i, "wax"))
            conv_bd(w_x[j], pbd[:, 2048:4096], (i, "wax"))
            conv_dense(w_out[j], lambda pj, i=i: (i, "wout"), 1024)
        else:
            pbd = scr_piece((i, "pw"))
            conv_bd(pool_w[j], pbd[:, 0:2048], (i, "pw"))
        conv_dense(w1[i], lambda pj, i=i: (i, "w1", pj), FF)
        conv_w2(w2[i], lambda pj, i=i: (i, "w2", pj))

    while mod_tasks or conv_tasks:
        if mod_tasks:
            mod_tasks.pop(0)()
        for _ in range(2):
            if conv_tasks:
                conv_tasks.pop(0)()

    sch.extra_reads = (("der",), ("vecs",), ("kcol",), ("ones",), ("invc",))


    def use_piece(pkey, n=SLOTW):
        s = next_slot()
        dstp = scr_piece(pkey)
        A("sp", (lambda e: e.dma_start(out=wsl[:, s, 0:n], in_=dstp[:, 0:n])),
          reads=[("scr", sub) for sub in piece_sub[pkey]], writes=[("wslot", s)], dma=("slot", s))
        return s

    ctr = {"tn": 0, "sq": 0, "xr": 0}
    def rot(name):
        v = ctr[name] % (4 if name == "xr" else 2)
        ctr[name] += 1
        return v

    pe_pend = []
    pend_bufs = []
    PEND_DELAY = 2

    def drain_pend(keep):
        while len(pe_pend) > keep:
            pe_pend.pop(0)()
            pend_bufs.pop(0)

    def mm_group(b, lhs_fn, rhs_fn, nk, rkeys):
        for kc in range(nk):
            A("pe", (lambda e, kc=kc: e.matmul(ps[:, b, :], lhsT=lhs_fn(kc), rhs=rhs_fn(kc),
                                               start=(kc == 0), stop=(kc == nk - 1))),
              reads=rkeys(kc), writes=[("psum", b)])
        if len(pe_pend) > PEND_DELAY:
            pe_pend.pop(0)()
            pend_bufs.pop(0)

    stat_ctr = [0]

    def stat_begin():
        k = stat_ctr[0] % 2
        stat_ctr[0] += 1
        return {"k": k, "bank": 6 + k, "n": 0}

    def stat_chunk(ctx, bi, c):
        xb = xbuf[bi]
        si = ctr["sq"] % NSQ
        ctr["sq"] += 1
        assert si not in pend_bufs, "sq buffer still pending"
        pend_bufs.append(si)
        first = ctx["n"] == 0
        last = ctx["n"] == NCH - 1
        ctx["n"] += 1
        bank = ctx["bank"]
        A("act", (lambda e: e.activation(out=sq[:, si, :], in_=xb[:, c, :], func=AF.Square)),
          reads=[("x", bi, c)], writes=[("sq", si)])
        pe_pend.append(lambda: A("pe", (lambda e: e.matmul(ps[:, bank, :], lhsT=ones[:], rhs=sq[:, si, :], start=first, stop=last)),
                                 reads=[("sq", si)], writes=[("psum", bank)]))

    def stat_finish(ctx):
        k, bank = ctx["k"], ctx["bank"]
        assert ctx["n"] == NCH
        drain_pend(0)
        A("act", (lambda e: e.activation(out=stdt[:, k, :], in_=ps[:, bank, :], func=AF.Sqrt, bias=kcol[:, 0:1], scale=1.0)),
          reads=[("psum", bank)], writes=[("stdt", k)])
        A("dve", (lambda e: e.reciprocal(out=rstd[:, k, :], in_=stdt[:, k, :])),
          reads=[("stdt", k)], writes=[("rstd", k)])

    def norm_apply(ctx, bi, scfn, bifn, dstfn, keyfn):
        xb = xbuf[bi]
        k = ctx["k"]
        for c in range(NCH):
            ti = rot("tn")
            A("dve", (lambda e, c=c, ti=ti: e.tensor_tensor(out=tn[:, ti, :], in0=xb[:, c, :], in1=rstd[:, k, :], op=ALU.mult)),
              reads=[("x", bi, c), ("rstd", k)], writes=[("tn", ti)])
            A("act", (lambda e, c=c, ti=ti: e.activation(out=dstfn(c), in_=tn[:, ti, :], func=AF.Identity,
                                                          scale=scfn(c), bias=bifn(c))),
              reads=[("tn", ti)], writes=([keyfn(c), ("r2", keyfn(c)[1] + 1)] if keyfn(c)[0] == "r2" else [keyfn(c)]))

    zero_b = lambda c: kcol[:, 3:4]

    def lru_norm(i, bi, ctx):
        norm_apply(ctx, bi, lambda c: Dc(dmod(i, 6), c), lambda c: Dc(dmod(i, 0), c),
                   lambda c: xn[:, c, :], lambda c: ("xn", c))

    def lru_mixer(i, bi, ctx, post, mask_state=False, before_out=None):
        j = i // 2
        xb = xbuf[bi]
        s_in = use_piece((i, "win"))
        prev_pair = []
        for oc0 in range(0, NCH, 2):
            pair = []
            for oc in (oc0, oc0 + 1):
                b = next_bank()
                mm_group(b, lambda kc, oc=oc: wsl[:, s_in, kc * 1024 + oc * 128: kc * 1024 + (oc + 1) * 128],
                         lambda kc: xn[:, kc, :], NCH, lambda kc: [("wslot", s_in), ("xn", kc)])
                xi = rot("xr")
                pair.append((oc, xi))
                A("pool", (lambda e, oc=oc, xi=xi: e.tensor_copy(out=xr[:, xi, 0:3], in_=chalo[:, j, oc, 0:3])),
                  reads=[("chalo", j, oc)], writes=[("xr", xi)])
                A("act", (lambda e, oc=oc, xi=xi, b=b: e.activation(out=xr[:, xi, 3:3 + T], in_=ps[:, b, :], func=AF.Identity,
                                                                     bias=V(("b_in", j), oc), scale=1.0)),
                  reads=[("psum", b)], writes=[("xr", xi)])
                if mask_state:
                    A("pool", (lambda e, oc=oc, xi=xi: e.tensor_scalar(out=chalo[:, j, oc, 0:3], in0=xr[:, xi, T:T + 3], scalar1=flg[:, 0:1], scalar2=None, op0=ALU.mult)),
                      reads=[("xr", xi)], writes=[("chalo", j, oc)])
                else:
                    A("pool", (lambda e, oc=oc, xi=xi: e.tensor_copy(out=chalo[:, j, oc, 0:3], in_=xr[:, xi, T:T + 3])),
                      reads=[("xr", xi)], writes=[("chalo", j, oc)])
            for oc, xi in pair:
                A("dve", (lambda e, oc=oc, xi=xi: e.tensor_scalar(out=big[:, oc, 0:T], in0=xr[:, xi, 3:3 + T],
                                                                   scalar1=V(("cw3", j), oc), scalar2=V(("conv_b", j), oc),
                                                                   op0=ALU.mult, op1=ALU.add)),
                  reads=[("xr", xi)], writes=[("big", oc)])
            for k in (2, 1, 0):
                for oc, xi in pair:
                    A("dve", (lambda e, oc=oc, xi=xi, k=k: e.scalar_tensor_tensor(
                        out=big[:, oc, 0:T], in0=xr[:, xi, k:k + T], scalar=V(("cw%d" % k, j), oc), in1=big[:, oc, 0:T],
                        op0=ALU.mult, op1=ALU.add)),
                      reads=[("xr", xi), ("big", oc)], writes=[("big", oc)])
            for oc_ in prev_pair:
                A("pool", (lambda e, oc=oc_: e.tensor_copy(out=cvb[:, oc, :], in_=big[:, oc, 0:T])),
                  reads=[("big", oc_)], writes=[("cvb", oc_)])
            prev_pair = [oc for oc, xi in pair]
        for oc_ in prev_pair:
            A("pool", (lambda e, oc=oc_: e.tensor_copy(out=cvb[:, oc, :], in_=big[:, oc, 0:T])),
              reads=[("big", oc_)], writes=[("cvb", oc_)])
        s_y = use_piece((i, "wy"))
        for oc in range(NCH):
            b = next_bank()
            mm_group(b, lambda kc, oc=oc: wsl[:, s_y, kc * 1024 + oc * 128: kc * 1024 + (oc + 1) * 128],
                     lambda kc: xn[:, kc, :], NCH, lambda kc: [("wslot", s_y), ("xn", kc)])
            A("act", (lambda e, oc=oc, b=b: e.activation(out=gate[:, oc, :], in_=ps[:, b, :], func=AF.Gelu_apprx_tanh,
                                                          bias=V(("b_y", j), oc), scale=1.0)),
              reads=[("psum", b)], writes=[("gate", oc)])
        s_ax = use_piece((i, "wax"), 4096)

        def gb(name, hd, q):
            if hd % 2 == 0:
                ap = {"tha": g_tha[:, q, 0:T], "thx": g_thx[:, q, 0:T], "a": g_a[:, q, :], "m": g_m[:, q, :]}[name]
                return ap, [({"tha": "g_tha", "thx": "g_thx", "a": "g_a", "m": "g_m"}[name], q)]
            idx = {"tha": 0, "thx": 1, "a": 2, "m": 3}[name] * 2 + q
            ap = r2[:, 2 * idx:2 * idx + 2, :].rearrange("p a t -> p (a t)").bitcast(F32)
            return ap, [("r2", 2 * idx), ("r2", 2 * idx + 1)]

        for hd in range(4):
            B = {(nm, q): gb(nm, hd, q) for nm in ("tha", "thx", "a", "m") for q in range(2)}
            for q in range(2):
                oc = 2 * hd + q
                ba = next_bank()
                bx = next_bank()
                mm_group(ba, lambda kc, q=q, hd=hd: wsl[:, s_ax, (hd * 2 + kc) * 256 + q * 128:(hd * 2 + kc) * 256 + (q + 1) * 128],
                         lambda kc, hd=hd: cvb[:, 2 * hd + kc, :], 2, lambda kc, hd=hd: [("wslot", s_ax), ("cvb", 2 * hd + kc)])
                mm_group(bx, lambda kc, q=q, hd=hd: wsl[:, s_ax, 2048 + (hd * 2 + kc) * 256 + q * 128:2048 + (hd * 2 + kc) * 256 + (q + 1) * 128],
                         lambda kc, hd=hd: cvb[:, 2 * hd + kc, :], 2, lambda kc, hd=hd: [("wslot", s_ax), ("cvb", 2 * hd + kc)])
                (tha, ktha), (thx, kthx) = B[("tha", q)], B[("thx", q)]
                A("act", (lambda e, oc=oc, ba=ba, tha=tha: e.activation(out=tha, in_=ps[:, ba, :], func=AF.Tanh,
                                                                         scale=0.5, bias=Dc(dlru(j, 0), oc))),
                  reads=[("psum", ba)], writes=ktha)
                A("act", (lambda e, oc=oc, bx=bx, thx=thx: e.activation(out=thx, in_=ps[:, bx, :], func=AF.Tanh,
                                                                         scale=0.5, bias=Dc(dlru(j, 1), oc))),
                  reads=[("psum", bx)], writes=kthx)
            for q in range(2):
                oc = 2 * hd + q
                (tha, ktha), (ga, ka), (gm, km) = B[("tha", q)], B[("a", q)], B[("m", q)]
                A("act", (lambda e, oc=oc, tha=tha, ga=ga: e.activation(out=ga, in_=tha, func=AF.Exp,
                                                                         scale=Dc(dlru(j, 2), oc), bias=Dc(dlru(j, 2), oc))),
                  reads=ktha, writes=ka)
                A("act", (lambda e, oc=oc, tha=tha, gm=gm: e.activation(out=gm, in_=tha, func=AF.Exp,
                                                                         scale=Dc(dlru(j, 3), oc), bias=Dc(dlru(j, 3), oc))),
                  reads=ktha, writes=km)
            for q in range(2):
                gm, km = B[("m", q)]
                A("act", (lambda e, gm=gm: e.activation(out=gm, in_=gm, func=AF.Sqrt, scale=-0.25, bias=kcol[:, 1:2])),
                  reads=km, writes=km)
            for q in range(2):
                oc = 2 * hd + q
                thx, kthx = B[("thx", q)]
                A("dve", (lambda e, oc=oc, thx=thx: e.scalar_tensor_tensor(out=thx, in0=thx, scalar=1.0,
                                                                            in1=big[:, oc, 0:T], op0=ALU.add, op1=ALU.mult)),
                  reads=kthx + [("big", oc)], writes=kthx)
            for q in range(2):
                (thx, kthx), (gm, km) = B[("thx", q)], B[("m", q)]
                A("dve", (lambda e, thx=thx, gm=gm: e.tensor_tensor(out=thx, in0=thx, in1=gm, op=ALU.mult)),
                  reads=kthx + km, writes=kthx)
            for q in range(2):
                oc = 2 * hd + q
                (thx, kthx), (gm, km), (ga, ka) = B[("thx", q)], B[("m", q)], B[("a", q)]
                A("dve", (lambda e, oc=oc, thx=thx, gm=gm, ga=ga: e.tensor_tensor_scan(out=gm, data0=ga, data1=thx,
                                                                                       initial=hst[:, j, oc:oc + 1], op0=ALU.mult, op1=ALU.add)),
                  reads=ka + kthx + [("hst", j, oc)], writes=km)
            for q in range(2):
                oc = 2 * hd + q
                gm, km = B[("m", q)]
                if mask_state:
                    A("pool", (lambda e, oc=oc, gm=gm: e.tensor_scalar(out=hst[:, j, oc:oc + 1], in0=gm[:, T - 1:T], scalar1=flg[:, 0:1], scalar2=None, op0=ALU.mult)),
                      reads=km, writes=[("hst", j, oc)])
                else:
                    A("pool", (lambda e, oc=oc, gm=gm: e.tensor_copy(out=hst[:, j, oc:oc + 1], in_=gm[:, T - 1:T])),
                      reads=km, writes=[("hst", j, oc)])
                A("dve", (lambda e, oc=oc, gm=gm: e.tensor_tensor(out=xn[:, oc, :], in0=gm, in1=gate[:, oc, :], op=ALU.mult)),
                  reads=km + [("gate", oc)], writes=[("xn", oc)])
        if before_out is not None:
            before_out()
        s_o = use_piece((i, "wout"))
        for oc in range(NCH):
            b = next_bank()
            mm_group(b, lambda kc, oc=oc: wsl[:, s_o, kc * 1024 + oc * 128: kc * 1024 + (oc + 1) * 128],
                     lambda kc: xn[:, kc, :], NCH, lambda kc: [("wslot", s_o), ("xn", kc)])
            ti = rot("tn")
            A("act", (lambda e, oc=oc, ti=ti, b=b: e.activation(out=tn[:, ti, :], in_=ps[:, b, :], func=AF.Identity,
                                                                 scale=Dc(dmod(i, 2), oc), bias=Dc(dlru(j, 4), oc))),
              reads=[("psum", b)], writes=[("tn", ti)])
            A("pool", (lambda e, oc=oc, ti=ti: e.tensor_tensor(out=xb[:, oc, :], in0=xb[:, oc, :], in1=tn[:, ti, :], op=ALU.add)),
              reads=[("x", bi, oc), ("tn", ti)], writes=[("x", bi, oc)])
            post(oc)

    def pool_mixer(i, bi, ctx, post, tabw=None, mask_state=False):
        first = tabw is not None
        j = i // 2
        xb = xbuf[bi]
        for c in range(NCH):
            A("pool", (lambda e, c=c: e.tensor_copy(out=big[:, c, 0:HALO], in_=vhalo[:, j, c, :])),
              reads=[("vhalo", j, c)], writes=[("big", c)])
        norm_apply(ctx, bi, lambda c: Dc(dmod(i, 6), c), zero_b,
                   lambda c: big[:, c, HALO:HALO + T], lambda c: ("big", c))
        for c in range(NCH):
            if mask_state:
                A("pool", (lambda e, c=c: e.tensor_scalar(out=vhalo[:, j, c, :], in0=big[:, c, T:T + HALO], scalar1=flg[:, 0:1], scalar2=None, op0=ALU.mult)),
                  reads=[("big", c)], writes=[("vhalo", j, c)])
            else:
                A("pool", (lambda e, c=c: e.tensor_copy(out=vhalo[:, j, c, :], in_=big[:, c, T:T + HALO])),
                  reads=[("big", c)], writes=[("vhalo", j, c)])
        for c0 in range(0, NCH, 2):
            g = c0 // 2
            win = 2 << g
            st = []
            for w_, c in enumerate((c0, c0 + 1)):
                st.append({"c": c, "src": big[:, c, :], "skey": ("big", c), "buf": (pl, "g_tha") if w_ == 0 else (plb, "g_thx"), "ti": w_})
            for l in range(1, g + 2):
                sh = 1 << (l - 1)
                lo = 1 << l
                di = l % 2
                for d in st:
                    dst = d["buf"][0][:, di, :]
                    dkey = (d["buf"][1], di)
                    A("dve", (lambda e, src=d["src"], dst=dst, lo=lo, sh=sh: e.tensor_tensor(
                        out=dst[:, lo:HALO + T], in0=src[:, lo:HALO + T], in1=src[:, lo - sh:HALO + T - sh], op=ALU.add)),
                      reads=[d["skey"]], writes=[dkey])
                    d["src"], d["skey"] = dst, dkey
            for d in st:
                A("dve", (lambda e, c=d["c"], src=d["src"], win=win: e.scalar_tensor_tensor(
                    out=xn[:, c, :], in0=src[:, HALO:HALO + T], scalar=1.0 / win, in1=big[:, c, HALO:HALO + T],
                    op0=ALU.mult, op1=ALU.subtract)),
                  reads=[d["skey"], ("big", d["c"])], writes=[("xn", d["c"])])
            if first:
                for d in st:
                    A("dve", (lambda e, g=g, src=d["src"], ti=d["ti"]: e.tensor_tensor(out=tn[:, ti, 0:HALO], in0=src[:, HALO:2 * HALO],
                                                                                   in1=tabs[:, tabw, g, :], op=ALU.mult)),
                      reads=[d["skey"]], writes=[("tn", d["ti"])])
                for d in st:
                    A("dve", (lambda e, c=d["c"], ti=d["ti"]: e.tensor_tensor(out=xn[:, c, 0:HALO], in0=tn[:, ti, 0:HALO],
                                                                           in1=big[:, c, HALO:2 * HALO], op=ALU.subtract)),
                      reads=[("tn", d["ti"]), ("big", d["c"])], writes=[("xn", d["c"])])
        s_p = use_piece((i, "pw"), 2048)
        for oc in range(NCH):
            g, q = oc // 2, oc % 2
            b = next_bank()
            mm_group(b, lambda kc, g=g, q=q: wsl[:, s_p, (g * 2 + kc) * 256 + q * 128:(g * 2 + kc) * 256 + (q + 1) * 128],
                     lambda kc, g=g: xn[:, 2 * g + kc, :], 2, lambda kc, g=g: [("wslot", s_p), ("xn", 2 * g + kc)])
            A("dve", (lambda e, oc=oc, b=b: e.scalar_tensor_tensor(out=xb[:, oc, :], in0=ps[:, b, :], scalar=Dc(dpool(j), oc),
                                                                    in1=xb[:, oc, :], op0=ALU.mult, op1=ALU.add)),
              reads=[("psum", b), ("x", bi, oc)], writes=[("x", bi, oc)])
            post(oc)

    def ffn(i, bi, ctx, post, mid=None):
        xb = xbuf[bi]
        norm_apply(ctx, bi, lambda c: Dc(dmod(i, 7), c), lambda c: Dc(dmod(i, 3), c),
                   lambda c: xn[:, c, :], lambda c: ("xn", c))
        for pj in range(4):
            s = use_piece((i, "w1", pj))
            for q in range(8):
                hc = pj * 8 + q
                b = next_bank()
                mm_group(b, lambda kc, q=q, s=s: wsl[:, s, kc * 1024 + q * 128: kc * 1024 + (q + 1) * 128],
                         lambda kc: xn[:, kc, :], NCH, lambda kc, s=s: [("wslot", s), ("xn", kc)])
                ti = rot("tn")
                A("act", (lambda e, ti=ti, b=b: e.activation(out=tn[:, ti, :], in_=ps[:, b, :], func=AF.Relu)),
                  reads=[("psum", b)], writes=[("tn", ti)])
                eng = "dve" if hc % 2 == 0 else "pool"
                A(eng, (lambda e, ti=ti, hc=hc: e.tensor_tensor(out=r2[:, hc, :], in0=tn[:, ti, :], in1=tn[:, ti, :], op=ALU.mult)),
                  reads=[("tn", ti)], writes=[("r2", hc)])
        if mid is not None:
            mid()
        for pj in range(4):
            s = use_piece((i, "w2", pj))
            for q in range(2):
                oc = pj * 2 + q
                b = next_bank()
                mm_group(b, lambda hc, q=q, s=s: wsl[:, s, hc * 256 + q * 128: hc * 256 + (q + 1) * 128],
                         lambda hc: r2[:, hc, :], HC, lambda hc, s=s: [("wslot", s), ("r2", hc)])
                A("dve", (lambda e, oc=oc, b=b: e.scalar_tensor_tensor(out=xb[:, oc, :], in0=ps[:, b, :], scalar=Dc(dmod(i, 5), oc),
                                                                        in1=xb[:, oc, :], op0=ALU.mult, op1=ALU.add)),
                  reads=[("psum", b), ("x", bi, oc)], writes=[("x", bi, oc)])
                post(oc)

    def load_x(p):
        bi = p % 2
        if p < LAG:
            A("act", (lambda e: e.dma_start(out=xbuf[bi][:], in_=xT3[:, :, p * T:(p + 1) * T])),
              writes=[("x", bi, c) for c in range(NCH)], dma=("xin", bi))
        else:
            k = p - LAG
            A("act", (lambda e: e.dma_start(out=xbuf[bi][:], in_=cc_out[k].rearrange("p (c t) -> p c t", c=NCH))),
              reads=[("ccout", k)], writes=[("x", bi, c) for c in range(NCH)], dma=("xin", bi))

    def final(bi, jt, ctx):
        ost = lambda c: r2[:, 2 * c:2 * c + 2, :].rearrange("p a t -> p (a t)").bitcast(F32)
        xb = xbuf[bi]
        k = ctx["k"]
        for c in range(NCH):
            A("dve", (lambda e, c=c: e.scalar_tensor_tensor(out=ost(c), in0=xb[:, c, :], scalar=V(("fg",), c), in1=rstd[:, k, :],
                                                             op0=ALU.mult, op1=ALU.mult)),
              reads=[("x", bi, c), ("rstd", k)], writes=[("r2", 2 * c), ("r2", 2 * c + 1)])
        A("act", (lambda e: e.dma_start(out=oT3[:, :, jt * T:(jt + 1) * T],
                                        in_=r2[:, 0:2 * NCH, :].rearrange("p a t -> p (a t)").bitcast(F32).rearrange("p (c t) -> p c t", c=NCH))),
          reads=[("r2", q) for q in range(2 * NCH)], writes=[("out", jt)], dma=("out",))

    def send(bi, k):
        xb = xbuf[bi]
        for c in range(NCH):
            A("act", (lambda e, c=c: e.activation(out=xb[:, c, :], in_=xb[:, c, :], func=AF.Identity, scale=flg[:, 0:1], bias=kcol[:, 3:4])),
              reads=[("x", bi, c)], writes=[("x", bi, c)])
        A("act", (lambda e: e.dma_start(out=cc_in[k][128:256, :].rearrange("p (c t) -> p c t", c=NCH), in_=xb[:])),
          reads=[("x", bi, c) for c in range(NCH)], writes=[("ccin1", k)], dma=("send", k % 2))
        A("pool", (lambda e: e.collective_compute("ReduceScatter", ALU.add, replica_groups=GROUPS,
                                                  ins=[cc_in[k].opt()], outs=[cc_out[k].opt()])),
          reads=[("ccin0",), ("ccin1", k)], writes=[("ccout", k)], dma="cc", inc=1)

    sch.extra_reads = (("der",), ("vecs",), ("kcol",), ("ones",), ("invc",), ("hst",), ("chalo",), ("vhalo",), ("tabs",), ("flg",))
    load_x(0)
    if LAG > 1:
        load_x(1)
    ctx0 = stat_begin()
    for c in range(NCH):
        stat_chunk(ctx0, 0, c)
    stat_finish(ctx0)
    nothing = lambda oc: None
    deferred = None
    lru_norm(0, 0, ctx0)
    for p in range(NP):
        bi = p % 2
        ms = p < LAG
        if 1 < p + 1 < LAG:
            load_x(p + 1)
        ctxA = stat_begin()
        lru_mixer(0, bi, ctx0, (lambda oc, ctxA=ctxA: stat_chunk(ctxA, bi, oc)), mask_state=ms, before_out=deferred)
        deferred = None
        stat_finish(ctxA)
        ctxB = stat_begin()
        ffn(0, bi, ctxA, (lambda oc, ctxB=ctxB: stat_chunk(ctxB, bi, oc)))
        stat_finish(ctxB)
        if p + 1 < NP and p + 1 >= LAG:
            load_x(p + 1)
        ctxC = stat_begin()
        pool_mixer(1, bi, ctxB, (lambda oc, ctxC=ctxC: stat_chunk(ctxC, bi, oc)),
                   tabw=(0 if p == 0 else (1 if p == LAG else None)), mask_state=ms)
        stat_finish(ctxC)
        nxt = {}

        def mid(p=p, nxt=nxt):
            if p + 1 < NP:
                c0 = stat_begin()
                for c in range(NCH):
                    stat_chunk(c0, (p + 1) % 2, c)
                nxt["ctx"] = c0
        do_final = p >= LAG
        ctxF = {}

        def mid2(p=p):
            mid()
            if do_final:
                ctxF["ctx"] = stat_begin()
        ffn(1, bi, ctxC, (lambda oc: stat_chunk(ctxF["ctx"], bi, oc)) if do_final else nothing, mid=mid2)
        if "ctx" in nxt:
            stat_finish(nxt["ctx"])
            lru_norm(0, (p + 1) % 2, nxt["ctx"])
        if do_final:
            stat_finish(ctxF["ctx"])

        def deferred(p=p, bi=bi, do_final=do_final, cf=ctxF.get("ctx")):
            if do_final:
                final(bi, p - LAG, cf)
            if p < NCC:
                send(bi, p)
        ctx0 = nxt.get("ctx")
    if deferred is not None:
        deferred()

    if debug:
        dbg_der = nc.dram_tensor("dbg_der", [128, ND * NCH], F32, kind="ExternalOutput").ap()
        dbg_gate = nc.dram_tensor("dbg_gate", [128, NCH * T], BF16, kind="ExternalOutput").ap()
        dbg_cvb = nc.dram_tensor("dbg_cvb", [128, NCH * T], BF16, kind="ExternalOutput").ap()
        dbg_r2 = nc.dram_tensor("dbg_r2", [128, HC * T], BF16, kind="ExternalOutput").ap()
        dbg_x = nc.dram_tensor("dbg_x", [128, NCH * T], F32, kind="ExternalOutput").ap()
        A("act", (lambda e: e.dma_start(out=dbg_der, in_=der[:].rearrange("p a b -> p (a b)"))), reads=[("der",)], dma=("dbg", 0))
        A("act", (lambda e: e.dma_start(out=dbg_gate, in_=gate[:].rearrange("p a b -> p (a b)"))), reads=[("gate", c) for c in range(NCH)], dma=("dbg", 1))
        A("act", (lambda e: e.dma_start(out=dbg_cvb, in_=cvb[:].rearrange("p a b -> p (a b)"))), reads=[("cvb", c) for c in range(NCH)], dma=("dbg", 2))
        A("act", (lambda e: e.dma_start(out=dbg_r2, in_=r2[:].rearrange("p a b -> p (a b)"))), reads=[("r2", c) for c in range(HC)], dma=("dbg", 3))
        A("act", (lambda e: e.dma_start(out=dbg_x, in_=xbuf[(NP - 1) % 2][:].rearrange("p a b -> p (a b)"))), reads=[("x", (NP - 1) % 2, c) for c in range(NCH)], dma=("dbg", 4))
    sch.assign(nc, stack)
    with nc.Block() as block:
        @block.tensor
        def _(e):
            sch.emit_engine("pe", e)

        @block.vector
        def _(e):
            sch.emit_engine("dve", e)

        @block.gpsimd
        def _(e):
            sch.emit_engine("pool", e)

        @block.sync
        def _(e):
            sch.emit_engine("sp", e)

        @block.scalar
        def _(e):
            sch.emit_engine("act", e, tail_waits=sch.dma_final)
    stack.close()
    return nc


def _pk(v):
    return np.ascontiguousarray(np.asarray(v, np.float32).reshape(NCH, 128).T)


def prep_inputs(inputs, NT=SEQ // T):
    f = lambda k: np.asarray(inputs[k], np.float32)
    bm = f("b_mod")
    x = f("x")
    c = f("c")
    stage_shared = []
    for s in range(2):
        vecs = np.zeros((128, NV, NCH), np.float32)
        for il in range(2):
            i = 2 * s + il
            vecs[:, VIDX[("gmix", il)]] = _pk(f("norm_mix_g")[i])
            vecs[:, VIDX[("gffn", il)]] = _pk(f("norm_ffn_g")[i])
            for k in range(6):
                vecs[:, VIDX[("bmod", il, k)]] = _pk(bm[i, k * D:(k + 1) * D])
        for nm, key in (("b_y", "lru_b_y"), ("b_in", "lru_b_in"), ("conv_b", "lru_conv_b"), ("b_a", "lru_b_a"),
                        ("b_x", "lru_b_x"), ("lam", "lru_lambda"), ("b_out", "lru_b_out")):
            vecs[:, VIDX[(nm, 0)]] = _pk(f(key)[s].reshape(-1))
        for k in range(4):
            vecs[:, VIDX[("cw%d" % k, 0)]] = _pk(f("lru_conv_w")[s, k])
        vecs[:, VIDX[("pscale", 0)]] = _pk(f("pool_scale")[s])
        vecs[:, VIDX[("fg",)]] = _pk(f("final_norm_g"))
        flags = np.zeros((128, 4), np.float32)
        flags[:, s] = 1.0
        stage_shared.append({
            "vecs": np.ascontiguousarray(vecs.reshape(128, NV * NCH)),
            "flags": flags,
            "w_mod": np.ascontiguousarray(f("w_mod")[2 * s:2 * s + 2]),
            "lru_w_y": np.ascontiguousarray(f("lru_w_y")[s:s + 1]),
            "lru_w_in": np.ascontiguousarray(f("lru_w_in")[s:s + 1]),
            "lru_w_out": np.ascontiguousarray(f("lru_w_out")[s:s + 1]),
            "lru_w_a": np.ascontiguousarray(f("lru_w_a")[s:s + 1]),
            "lru_w_x": np.ascontiguousarray(f("lru_w_x")[s:s + 1]),
            "pool_w": np.ascontiguousarray(f("pool_w")[s:s + 1]),
            "ffn_w1": np.ascontiguousarray(f("ffn_w1")[2 * s:2 * s + 2]),
            "ffn_w2": np.ascontiguousarray(f("ffn_w2")[2 * s:2 * s + 2]),
        })
    zeros_xT = np.zeros((D, NT * T), np.float32)
    maps = []
    for b in range(x.shape[0]):
        for s in range(2):
            m = dict(stage_shared[s])
            m["xT"] = np.ascontiguousarray(x[b, :NT * T].T) if s == 0 else zeros_xT
            m["cvec"] = _pk(c[b])
            maps.append(m)
    return maps


_NC_CACHE = {}


def kernel(**inputs):
    NT = SEQ // T
    maps = prep_inputs(inputs, NT)
    if NT not in _NC_CACHE:
        _NC_CACHE[NT] = build_program(NT)
    nc = _NC_CACHE[NT]
    res = run_bass_kernel_spmd(nc, maps, core_ids=list(range(8)))
    out = np.stack([np.ascontiguousarray(res.results[2 * b + 1]["outT"].T) for b in range(BATCH)])
    return out.astype(np.float32)
```
